# Optimizing a Trainium2 kernel written in Bass

```python
import math
import jax, jax.numpy as jnp
from jax import lax
import numpy as np

D_MODEL = 1024
BATCH = 2
SEQ = 8192
DEPTH = 1

N_META = 16
BLOCK = 128
WINDOW = 128
HEAD_DIM = 64
A_Q_HEADS = 8
A_KV_HEADS = 2
A_WIDTH = A_Q_HEADS * HEAD_DIM
B_HEADS = 4
B_V_DIM = 2 * HEAD_DIM
B_WIDTH = B_HEADS * B_V_DIM
RMS_EPS = 1e-6
NEG = -1e30

IN_SIZES = [
    A_Q_HEADS * HEAD_DIM,
    A_KV_HEADS * HEAD_DIM,
    A_KV_HEADS * HEAD_DIM,
    A_WIDTH,
    B_HEADS * 2 * HEAD_DIM,
    B_HEADS * 2 * HEAD_DIM,
    B_WIDTH,
    B_WIDTH,
    2 * D_MODEL,
]
IN_DIM = sum(IN_SIZES)
IN_SPLITS = [int(c) for c in np.cumsum(IN_SIZES)[:-1]]

kernel_name = "hybrid_gated_swa_sink_diffattn_alibi"


def rms_norm(x, g):
    xf = x.astype(jnp.float32)
    y = xf * lax.rsqrt(jnp.mean(xf * xf, axis=-1, keepdims=True) + RMS_EPS)
    return (y * g.astype(jnp.float32)).astype(x.dtype)


def alibi_slopes(n_heads):
    return jnp.exp2(-8.0 * jnp.arange(1, n_heads + 1, dtype=jnp.float32) / n_heads)


def sliding_window_gqa(q, k, v, qn, kn, sink, pos):
    b, lp = q.shape[:2]
    nb = lp // BLOCK
    grp = A_Q_HEADS // A_KV_HEADS
    scale = HEAD_DIM ** -0.5
    q = rms_norm(q, qn)
    k = rms_norm(k, kn)
    qb = q.reshape(b, nb, BLOCK, A_KV_HEADS, grp, HEAD_DIM)
    kb = k.reshape(b, nb, BLOCK, A_KV_HEADS, HEAD_DIM)
    vb = v.reshape(b, nb, BLOCK, A_KV_HEADS, HEAD_DIM)

    def with_prev(t):
        prev = jnp.pad(t, ((0, 0), (1, 0), (0, 0), (0, 0), (0, 0)))[:, :-1]
        return jnp.concatenate([prev, t], axis=2)

    kk, vv = with_prev(kb), with_prev(vb)
    qpos = pos.reshape(nb, BLOCK)
    kpos = jnp.concatenate([qpos - BLOCK, qpos], axis=1)
    dist = qpos[:, :, None] - kpos[:, None, :]
    valid = (dist >= 0) & (dist < WINDOW) & (kpos[:, None, :] >= 0)
    slopes = alibi_slopes(A_Q_HEADS).reshape(1, 1, A_KV_HEADS, grp, 1, 1)

    s = jnp.einsum('bnqkgd,bnskd->bnkgqs', qb, kk).astype(jnp.float32) * scale
    s = s - slopes * dist.astype(jnp.float32)[None, :, None, None]
    s = jnp.where(valid[None, :, None, None], s, NEG)
    snk = sink.astype(jnp.float32).reshape(1, 1, A_KV_HEADS, grp, 1, 1)
    m = jnp.maximum(jnp.max(s, axis=-1, keepdims=True), snk)
    p = jnp.exp(s - m)
    probs = p / (jnp.sum(p, axis=-1, keepdims=True) + jnp.exp(snk - m))
    o = jnp.einsum('bnkgqs,bnskd->bnqkgd', probs.astype(v.dtype), vv)
    return o.reshape(b, lp, A_WIDTH)


def diff_attention(q, k, v, qn, kn, lam, subln, lam_init, pos):
    b, lp = q.shape[:2]
    nb = lp // BLOCK
    scale = HEAD_DIM ** -0.5
    q = rms_norm(q, qn)
    k = rms_norm(k, kn)
    slopes = alibi_slopes(B_HEADS)[None, :, None, None, None]
    qblocks = jnp.moveaxis(q.reshape(b, nb, BLOCK, B_HEADS, 2, HEAD_DIM), 1, 0)
    qpos = pos.reshape(nb, BLOCK)

    def one_block(args):
        qb, qp = args
        s = jnp.einsum('bqhmd,bshmd->bhmqs', qb, k).astype(jnp.float32) * scale
        dist = qp[:, None] - pos[None, :]
        valid = (dist >= 0) & (pos[None, :] >= 0)
        s = jnp.where(valid, s - slopes * dist.astype(jnp.float32), NEG)
        a = jax.nn.softmax(s, axis=-1)
        w = a[:, :, 0] - lam * a[:, :, 1]
        return jnp.einsum('bhqs,bshd->bqhd', w.astype(v.dtype), v)

    o = lax.map(one_block, (qblocks, qpos))
    o = jnp.moveaxis(o, 0, 1).reshape(b, lp, B_HEADS, B_V_DIM)
    o = rms_norm(o, subln) * (1.0 - lam_init)
    return o.reshape(b, lp, B_WIDTH)


def setup_inputs(seed: int = 0) -> dict:
    key = jax.random.key(seed)
    ks = jax.random.split(key, 17)
    f32 = jnp.float32
    nrm = lambda k, shape, s: jax.random.normal(k, shape, f32) * s
    return {
        "x": nrm(ks[0], (BATCH, SEQ, D_MODEL), 1.0),
        "meta": nrm(ks[1], (N_META, D_MODEL), 1.0),
        "norm_g": 1.0 + nrm(ks[2], (DEPTH, D_MODEL), 0.05),
        "w_in": nrm(ks[3], (DEPTH, D_MODEL, IN_DIM), D_MODEL ** -0.5),
        "a_qn": 1.0 + nrm(ks[4], (DEPTH, HEAD_DIM), 0.05),
        "a_kn": 1.0 + nrm(ks[5], (DEPTH, HEAD_DIM), 0.05),
        "a_sink": nrm(ks[6], (DEPTH, A_Q_HEADS), 1.0),
        "b_qn": 1.0 + nrm(ks[7], (DEPTH, HEAD_DIM), 0.05),
        "b_kn": 1.0 + nrm(ks[8], (DEPTH, HEAD_DIM), 0.05),
        "b_lq1": nrm(ks[9], (DEPTH, HEAD_DIM), 0.1),
        "b_lk1": nrm(ks[10], (DEPTH, HEAD_DIM), 0.1),
        "b_lq2": nrm(ks[11], (DEPTH, HEAD_DIM), 0.1),
        "b_lk2": nrm(ks[12], (DEPTH, HEAD_DIM), 0.1),
        "b_subln": 1.0 + nrm(ks[13], (DEPTH, B_V_DIM), 0.05),
        "w_up_a": nrm(ks[14], (DEPTH, A_WIDTH, D_MODEL), A_WIDTH ** -0.5),
        "w_up_b": nrm(ks[15], (DEPTH, B_WIDTH, D_MODEL), B_WIDTH ** -0.5),
        "w_o": nrm(ks[16], (DEPTH, D_MODEL, D_MODEL), D_MODEL ** -0.5),
    }


def reference(x, meta, norm_g, w_in, a_qn, a_kn, a_sink, b_qn, b_kn, b_lq1, b_lk1, b_lq2, b_lk2,
              b_subln, w_up_a, w_up_b, w_o):
    b = x.shape[0]
    n_pad = BLOCK - N_META
    h = jnp.concatenate([
        jnp.zeros((b, n_pad, D_MODEL), x.dtype),
        jnp.broadcast_to(meta.astype(x.dtype)[None], (b, N_META, D_MODEL)),
        x,
    ], axis=1)
    lp = h.shape[1]
    pos = jnp.arange(lp, dtype=jnp.int32) - n_pad

    for l in range(DEPTH):
        lam_init = 0.8 - 0.6 * math.exp(-0.3 * l)
        u = rms_norm(h, norm_g[l])
        proj = u @ w_in[l]
        qa, ka, va, za, qb, kb, vb, zb, gl = jnp.split(proj, IN_SPLITS, axis=-1)

        ya = sliding_window_gqa(
            qa.reshape(b, lp, A_Q_HEADS, HEAD_DIM),
            ka.reshape(b, lp, A_KV_HEADS, HEAD_DIM),
            va.reshape(b, lp, A_KV_HEADS, HEAD_DIM),
            a_qn[l], a_kn[l], a_sink[l], pos)
        ya = ya * jax.nn.silu(za)

        lam = (jnp.exp(jnp.sum(b_lq1[l].astype(jnp.float32) * b_lk1[l].astype(jnp.float32)))
               - jnp.exp(jnp.sum(b_lq2[l].astype(jnp.float32) * b_lk2[l].astype(jnp.float32)))
               + lam_init)
        yb = diff_attention(
            qb.reshape(b, lp, B_HEADS, 2, HEAD_DIM),
            kb.reshape(b, lp, B_HEADS, 2, HEAD_DIM),
            vb.reshape(b, lp, B_HEADS, B_V_DIM),
            b_qn[l], b_kn[l], lam, b_subln[l], lam_init, pos)
        yb = yb * jax.nn.silu(zb)

        gates = jax.nn.sigmoid(gl.reshape(b, lp, 2, D_MODEL))
        mix = gates[:, :, 0] * (ya @ w_up_a[l]) + gates[:, :, 1] * (yb @ w_up_b[l])
        h = h + mix @ w_o[l]

    return h[:, BLOCK:]
```

```python
from contextlib import ExitStack

import numpy as np
import concourse.bass as bass
import concourse.mybir as mybir
from concourse.bass_utils import run_bass_kernel_spmd

F32 = mybir.dt.float32
BF16 = mybir.dt.bfloat16
AF = mybir.ActivationFunctionType
ALU = mybir.AluOpType
AX = mybir.AxisListType

D = 1024
KC = 8
EPS = 1e-6
N_META = 16
NPAD = 128 - N_META
LAM_INIT = 0.8 - 0.6 * 1.0
NEGBIG = -30000.0

C_QA, C_KA, C_VA, C_ZA, C_QB, C_KB, C_VB, C_ZB, C_GL = 0, 512, 640, 768, 1280, 1792, 2304, 2816, 3328

ENGS = ["sync", "scalar", "vector", "gpsimd", "tensor"]


class T:
    def __init__(self, ap, excl=False):
        self.ap = ap
        self.w = None
        self.r = {}
        self.excl = excl


class Prog:
    def __init__(self, nc, sems):
        self.nc = nc
        self.sems = sems
        self.cnt = {k: 0 for k in sems}
        self.waited = {}
        self.q = {e: [] for e in ENGS}

    def op(self, eng, fn, r=(), w=(), dsem=None, extra=(), sig=True):
        toks = []
        for t in r:
            if t.w is not None:
                toks.append(t.w)
            if t.excl:
                for n, v in t.r.items():
                    if n != eng:
                        toks.append((n, v))
        for t in w:
            if t.w is not None:
                toks.append(t.w)
            for n, v in t.r.items():
                toks.append((n, v))
        toks.extend([x for x in extra if x is not None])
        ws = {}
        for (n, v) in toks:
            if eng == "tensor" and n == "tensor":
                continue
            if self.waited.get((eng, n), -1) >= v:
                continue
            ws[n] = max(ws.get(n, -1), v)
        for n, v in ws.items():
            self.waited[(eng, n)] = v
        tok = None
        sg = None
        if sig:
            if dsem is not None:
                sg = (dsem, 16)
            else:
                sg = (eng, 1)
            self.cnt[sg[0]] += sg[1]
            tok = (sg[0], self.cnt[sg[0]])
            for t in w:
                t.w = tok
                t.r = {}
            for t in r:
                t.r[tok[0]] = max(t.r.get(tok[0], -1), tok[1])
        sems = self.sems
        wl = list(ws.items())

        def emit(e, fn=fn, wl=wl, sg=sg):
            for (n, v) in wl:
                e.wait_ge(sems[n], v)
            ins = fn(e)
            if sg is not None:
                ins.then_inc(sems[sg[0]], sg[1])
        self.q[eng].append(emit)
        return tok

    def grp(self, eng, fns, r=(), w=()):
        n = len(fns)
        tok = None
        for i, fn in enumerate(fns):
            if n == 1:
                tok = self.op(eng, fn, r=r, w=w)
            elif i == 0:
                self.op(eng, fn, r=r, w=w, sig=False)
            elif i == n - 1:
                tok = self.op(eng, fn, r=r, w=w)
            else:
                self.op(eng, fn, sig=False)
        return tok

    def run_block(self):
        q = self.q
        with self.nc.Block() as block:
            @block.sync
            def _(e):
                for f in q["sync"]:
                    f(e)

            @block.scalar
            def _(e):
                for f in q["scalar"]:
                    f(e)

            @block.vector
            def _(e):
                for f in q["vector"]:
                    f(e)

            @block.gpsimd
            def _(e):
                for f in q["gpsimd"]:
                    f(e)

            @block.tensor
            def _(e):
                for f in q["tensor"]:
                    f(e)
        self.q = {e: [] for e in ENGS}


def build_program(NCH=8, debug=False, nphase=3):
    NB = 1 + 8 * NCH
    NOWN = 2 * NCH
    nc = bass.Bass("TRN2", target_bir_lowering=False)

    def din(name, shape):
        return nc.dram_tensor(name, shape, F32, kind="ExternalInput").ap()

    xkv = din("xkv", [NB * 128, D])
    xq = din("xq", [NOWN * 128, D])
    xp = din("xp", [NOWN * 128, D])
    cvec = din("cvec", [128, 2])
    w_in = din("w_in", [D, 5376])
    w_up_a = din("w_up_a", [512, D])
    w_up_b = din("w_up_b", [512, D])
    w_o = din("w_o", [D, D])
    norm_g = din("norm_g", [1, D])
    a_qn = din("a_qn", [1, 64])
    a_kn = din("a_kn", [1, 64])
    a_sink = din("a_sink", [1, 8])
    b_qn = din("b_qn", [1, 64])
    b_kn = din("b_kn", [1, 64])
    b_lq1 = din("b_lq1", [1, 64])
    b_lk1 = din("b_lk1", [1, 64])
    b_lq2 = din("b_lq2", [1, 64])
    b_lk2 = din("b_lk2", [1, 64])
    b_subln = din("b_subln", [1, 128])
    out = nc.dram_tensor("out", [NOWN * 128, D], F32, kind="ExternalOutput").ap()
    dbg = {}
    if debug:
        dbg["d_kt"] = nc.dram_tensor("d_kt", [128, NB * 512], F32, kind="ExternalOutput").ap()
        dbg["d_v"] = nc.dram_tensor("d_v", [128, NB * 516], F32, kind="ExternalOutput").ap()
        dbg["d_yb"] = nc.dram_tensor("d_yb", [128, NOWN * 512], F32, kind="ExternalOutput").ap()

    sem_names = ["sync", "scalar", "vector", "gpsimd", "tensor",
                 "dx0", "dx1", "dx2", "dx3", "dp0", "dp1", "dw", "dw2", "dw3", "dw4", "dc", "do0", "do1", "dd"]

    with ExitStack() as es:
        sems = {n: es.enter_context(nc.semaphore(n)) for n in sem_names}
        P = Prog(nc, sems)

        def sb(scope, name, shape, dt):
            return scope.enter_context(nc.sbuf_tensor(name, shape, dt))

        YB = sb(es, "YB", [128, max(NOWN, 6), 512], BF16)
        g_rep = sb(es, "g_rep", [128, D], F32)
        ident = sb(es, "ident", [128, 128], BF16)
        cv = sb(es, "cv", [128, 2], F32)
        kio = sb(es, "kio", [128, 1], F32)
        gains = sb(es, "gains", [128, 4, 64], F32)
        lams = sb(es, "lams", [128, 8], F32)
        subg = sb(es, "subg", [128, 128], F32)
        sinkb = sb(es, "sinkb", [128, 8], F32)
        sinkexp = sb(es, "sinkexp", [128, 8], F32)
        NJ = 8 * NCH + 1
        biasB = sb(es, "biasB", [128, 4, NJ], F32)
        biasB0 = sb(es, "biasB0", [128, 4, NCH], F32)
        biasA = sb(es, "biasA", [128, 8, 2], F32)
        MASK = sb(es, "MASK", [128, 9, 128], BF16)
        Mown = sb(es, "Mown", [128, 128], BF16)
        Mprev = sb(es, "Mprev", [128, 128], BF16)
        kvalA = sb(es, "kvalA", [128, NOWN], F32)
        scr = YB[:].rearrange("p a c -> p (a c)").bitcast(F32)
        lamt = scr[:, 1280:1536].rearrange("p (a d) -> p a d", d=64)
        pad0 = sb(es, "pad0", [128, 1], F32)

        PP = [es.enter_context(nc.psum_tensor(f"pp{i}", [128, 2, 512], F32)) for i in range(4)]
        PSf = [PP[i // 2][:, i % 2, :] for i in range(8)]
        PS = [T(None, excl=True) for _ in range(8)]

        def psb(i):
            return PSf[i].bitcast(BF16)

        tKT = [T(None) for _ in range(NB)]
        tVS = [T(None) for _ in range(NB)]
        tYB = [T(None) for _ in range(NOWN)]
        tC = T(None)

        def cload(dst, src):
            P.op("sync", lambda e: e.dma_start(out=dst, in_=src), w=[tC], dsem="dc")

        cload(g_rep[:], norm_g.partition_broadcast(128))
        cload(cv[:], cvec)
        cload(gains[:, 0, :], a_qn.partition_broadcast(128))
        cload(gains[:, 1, :], a_kn.partition_broadcast(128))
        cload(gains[:, 2, :], b_qn.partition_broadcast(128))
        cload(gains[:, 3, :], b_kn.partition_broadcast(128))
        cload(lamt[:, 0, :], b_lq1.partition_broadcast(128))
        cload(lamt[:, 1, :], b_lk1.partition_broadcast(128))
        cload(lamt[:, 2, :], b_lq2.partition_broadcast(128))
        cload(lamt[:, 3, :], b_lk2.partition_broadcast(128))
        cload(subg[:], b_subln.partition_broadcast(128))
        cload(sinkb[:], a_sink.partition_broadcast(128))

        def cop(eng, fn):
            P.op(eng, fn, r=[tC], w=[tC])

        cop("gpsimd", lambda e: e.iota(kio[:], pattern=[[0, 1]], base=0, channel_multiplier=1,
                                      allow_small_or_imprecise_dtypes=True))
        cop("gpsimd", lambda e: e.iota(scr[:, 0:128], pattern=[[1, 128]], base=0, channel_multiplier=-1,
                                      allow_small_or_imprecise_dtypes=True))
        cop("vector", lambda e: e.tensor_scalar(out=ident[:], in0=scr[:, 0:128], scalar1=0.0, scalar2=None,
                                               op0=ALU.is_equal))
        cop("vector", lambda e: e.tensor_scalar(out=Mown[:], in0=scr[:, 0:128], scalar1=0.0, scalar2=None,
                                               op0=ALU.is_ge))
        cop("vector", lambda e: e.tensor_scalar(out=Mprev[:], in0=scr[:, 0:128], scalar1=0.0, scalar2=None,
                                               op0=ALU.is_lt))
        cop("gpsimd", lambda e: e.iota(scr[:, 0:9 * 128].rearrange("p (a q) -> p a q", q=128),
                                      pattern=[[128, 9], [1, 128]], base=-7 * 128, channel_multiplier=-1,
                                      allow_small_or_imprecise_dtypes=True))
        cop("vector", lambda e: e.tensor_scalar(out=MASK[:].rearrange("p a q -> p (a q)"), in0=scr[:, 0:9 * 128],
                                               scalar1=cv[:, 1:2], scalar2=0.0, op0=ALU.add, op1=ALU.is_ge))
        cop("vector", lambda e: e.tensor_scalar(out=pad0[:], in0=kio[:], scalar1=float(NPAD), scalar2=NEGBIG,
                                               op0=ALU.is_lt, op1=ALU.mult))
        cop("gpsimd", lambda e: e.iota(scr[:, 0:NJ], pattern=[[1, NJ]], base=-6, channel_multiplier=0,
                                      allow_small_or_imprecise_dtypes=True))
        cop("vector", lambda e: e.tensor_scalar(out=scr[:, 0:NJ], in0=scr[:, 0:NJ], scalar1=cv[:, 0:1], scalar2=0.0,
                                               op0=ALU.add, op1=ALU.max))
        cop("vector", lambda e: e.tensor_scalar(out=scr[:, 0:NJ], in0=scr[:, 0:NJ], scalar1=-128.0, scalar2=kio[:, 0:1],
                                               op0=ALU.mult, op1=ALU.add))
        for h in range(4):
            sl = 2.0 ** (-2 * (h + 1))
            cop("vector", lambda e, h=h, sl=sl: e.tensor_scalar(out=biasB[:, h, :], in0=scr[:, 0:NJ], scalar1=sl,
                                                               scalar2=None, op0=ALU.mult))
        cop("gpsimd", lambda e: e.iota(scr[:, 256:256 + NCH], pattern=[[8, NCH]], base=2, channel_multiplier=0,
                                      allow_small_or_imprecise_dtypes=True))
        cop("vector", lambda e: e.tensor_scalar(out=scr[:, 256:256 + NCH], in0=scr[:, 256:256 + NCH], scalar1=cv[:, 0:1],
                                               scalar2=-128.0, op0=ALU.add, op1=ALU.mult))
        cop("vector", lambda e: e.tensor_scalar(out=scr[:, 256:256 + NCH], in0=scr[:, 256:256 + NCH], scalar1=kio[:, 0:1],
                                               scalar2=None, op0=ALU.add))
        for h in range(4):
            sl = 2.0 ** (-2 * (h + 1))
            cop("vector", lambda e, h=h, sl=sl: e.tensor_scalar(out=biasB0[:, h, :], in0=scr[:, 256:256 + NCH], scalar1=sl,
                                                               scalar2=pad0[:, 0:1], op0=ALU.mult, op1=ALU.add))
        for hq in range(8):
            sl = 2.0 ** (-(hq + 1))
            for kblk in range(2):
                off = -64.0 - (128.0 if kblk == 0 else 0.0)
                cop("vector", lambda e, hq=hq, kblk=kblk, sl=sl, off=off: e.tensor_scalar(
                    out=biasA[:, hq, kblk:kblk + 1], in0=kio[:], scalar1=off, scalar2=sl, op0=ALU.add, op1=ALU.mult))
        for hq in range(8):
            sl = 2.0 ** (-(hq + 1))
            cop("vector", lambda e, hq=hq, sl=sl: e.tensor_scalar(out=sinkexp[:, hq:hq + 1], in0=kio[:], scalar1=-64.0,
                                                                 scalar2=sl, op0=ALU.add, op1=ALU.mult))
        cop("vector", lambda e: e.tensor_tensor(out=sinkexp[:], in0=sinkexp[:], in1=sinkb[:], op=ALU.add))
        cop("scalar", lambda e: e.activation(out=sinkexp[:], in_=sinkexp[:], func=AF.Exp))
        cop("gpsimd", lambda e: e.iota(scr[:, 512:512 + NOWN].rearrange("p (i t) -> p i t", t=2),
                                      pattern=[[1024, NCH], [128, 2]], base=-NPAD, channel_multiplier=1,
                                      allow_small_or_imprecise_dtypes=True))
        cop("vector", lambda e: e.tensor_scalar(out=kvalA[:], in0=scr[:, 512:512 + NOWN], scalar1=cv[:, 1:2], scalar2=0.0,
                                               op0=ALU.add, op1=ALU.is_ge))
        cop("vector", lambda e: e.tensor_tensor(out=lamt[:, 0, :], in0=lamt[:, 0, :], in1=lamt[:, 1, :], op=ALU.mult))
        cop("vector", lambda e: e.tensor_tensor(out=lamt[:, 2, :], in0=lamt[:, 2, :], in1=lamt[:, 3, :], op=ALU.mult))
        cop("vector", lambda e: e.reduce_sum(out=lams[:, 0:1], in_=lamt[:, 0, :], axis=AX.X))
        cop("vector", lambda e: e.reduce_sum(out=lams[:, 1:2], in_=lamt[:, 2, :], axis=AX.X))
        cop("scalar", lambda e: e.activation(out=lams[:, 2:4], in_=lams[:, 0:2], func=AF.Exp))
        cop("vector", lambda e: e.tensor_tensor(out=lams[:, 4:5], in0=lams[:, 3:4], in1=lams[:, 2:3], op=ALU.subtract))
        cop("vector", lambda e: e.tensor_scalar(out=lams[:, 4:5], in0=lams[:, 4:5], scalar1=-LAM_INIT, scalar2=None,
                                               op0=ALU.add))
        cop("vector", lambda e: e.tensor_scalar(out=subg[:], in0=subg[:], scalar1=1.0 - LAM_INIT, scalar2=None,
                                               op0=ALU.mult))

        def load_w(dst_tile, tw, col_slices, dsem="dw", o0=0):
            o = o0
            for (c0, n) in col_slices:
                src = w_in[:, c0:c0 + n].rearrange("(kc p) c -> p kc c", p=128)
                P.op("gpsimd", lambda e, o=o, n=n, src=src: e.dma_start(out=dst_tile[:, :, o:o + n], in_=src),
                     w=[tw], dsem=dsem)
                o += n

        def xproc(xsrc_ap, xt, txt, dsem, st, tst, xb, txb, junk, tjunk, xT, txT, bank):
            P.op("sync", lambda e: e.dma_start(out=xt[:], in_=xsrc_ap), w=[txt], dsem=dsem)
            P.op("scalar", lambda e: e.activation(out=junk, in_=xt[:], func=AF.Square, accum_out=st[:, 0:1]),
                 r=[txt], w=[tjunk, tst])
            P.op("vector", lambda e: e.tensor_scalar(out=st[:, 1:2], in0=st[:, 0:1], scalar1=1.0 / D, scalar2=EPS,
                                                    op0=ALU.mult, op1=ALU.add), r=[tst], w=[tst])
            P.op("scalar", lambda e: e.activation(out=st[:, 2:3], in_=st[:, 1:2], func=AF.Ln), r=[tst], w=[tst])
            P.op("scalar", lambda e: e.activation(out=st[:, 3:4], in_=st[:, 2:3], func=AF.Exp, scale=-0.5),
                 r=[tst], w=[tst])
            P.op("vector", lambda e: e.tensor_scalar(out=st[:, 4:5], in0=st[:, 3:4], scalar1=-1.0, scalar2=None,
                                                    op0=ALU.mult), r=[tst], w=[tst])
            P.op("vector", lambda e: e.tensor_scalar(out=st[:, 5:6], in0=st[:, 3:4], scalar1=st[:, 3:4], scalar2=1.0 / 64,
                                                    op0=ALU.mult, op1=ALU.mult), r=[tst], w=[tst])
            P.op("vector", lambda e: e.tensor_scalar(out=st[:, 6:7], in0=st[:, 2:3], scalar1=-0.5, scalar2=None,
                                                    op0=ALU.mult), r=[tst], w=[tst])
            P.op("gpsimd", lambda e: e.tensor_tensor(out=xb[:], in0=xt[:], in1=g_rep[:], op=ALU.mult),
                 r=[txt, tC], w=[txb])
            if xT is not None:
                xproc_pe(xb, txb, xT, txT, bank)

        def xproc_pe(xb, txb, xT, txT, bank):
            pv = psb(bank)
            P.grp("tensor", [lambda e, kc=kc: e.transpose(out=pv[:, kc * 128:(kc + 1) * 128],
                                                         in_=xb[:, kc * 128:(kc + 1) * 128], identity=ident[:])
                             for kc in range(KC)], r=[txb, tC], w=[PS[bank]])
            P.op("vector", lambda e: e.tensor_copy(out=xT[:].rearrange("p k t -> p (k t)"), in_=pv[:, 0:1024]),
                 r=[PS[bank]], w=[txT])

        def proj(xT, txT, W, tW, c0, n, bank, boff=0, first=True):
            P.grp("tensor", [lambda e, kc=kc: e.matmul(PSf[bank][:, boff:boff + n], lhsT=xT[:, kc, :],
                                                      rhs=W[:, kc, c0:c0 + n], start=(kc == 0 and first),
                                                      stop=(kc == KC - 1), skip_group_check=True)
                             for kc in range(KC)], r=[txT, tW], w=[PS[bank]])

        def qknorm(bank, boff, G, gain_idx, st, tst, sq, tsq, tmp, ttmp, f8, tf8, outb, toutb):
            n = G * 64
            src = PSf[bank][:, boff:boff + n]
            P.op("scalar", lambda e: e.activation(out=sq[:, 0:n], in_=src, func=AF.Square), r=[PS[bank]], w=[tsq])
            P.op("vector", lambda e: e.reduce_sum(out=f8[:, 0:G], in_=sq[:, 0:n].rearrange("p (g d) -> p g d", d=64),
                                                 axis=AX.X), r=[tsq], w=[tf8])
            P.op("vector", lambda e: e.tensor_tensor(out=tmp[:, 0:n].rearrange("p (g d) -> p g d", d=64),
                                                    in0=src.rearrange("p (g d) -> p g d", d=64),
                                                    in1=gains[:, gain_idx, :].unsqueeze(1).to_broadcast([128, G, 64]),
                                                    op=ALU.mult), r=[PS[bank], tC], w=[ttmp])
            P.op("vector", lambda e: e.tensor_scalar(out=f8[:, 8:8 + G], in0=f8[:, 0:G], scalar1=st[:, 5:6], scalar2=EPS,
                                                    op0=ALU.mult, op1=ALU.add), r=[tf8, tst], w=[tf8])
            P.op("scalar", lambda e: e.activation(out=f8[:, 16:16 + G], in_=f8[:, 8:8 + G], func=AF.Ln), r=[tf8], w=[tf8])
            P.op("scalar", lambda e: e.activation(out=f8[:, 24:24 + G], in_=f8[:, 16:16 + G], func=AF.Exp, scale=-0.5,
                                                  bias=st[:, 6:7]), r=[tf8, tst], w=[tf8])
            P.op("vector", lambda e: e.tensor_tensor(out=outb[:, 0:n].rearrange("p (g d) -> p g d", d=64),
                                                    in0=tmp[:, 0:n].rearrange("p (g d) -> p g d", d=64),
                                                    in1=f8[:, 24:24 + G].unsqueeze(2).to_broadcast([128, G, 64]),
                                                    op=ALU.mult), r=[ttmp, tf8], w=[toutb])

        with ExitStack() as sa:
            KT = sb(sa, "KT", [128, NB, 4, 128], BF16)
            VS = sb(sa, "VS", [128, NB, 4, 129], BF16)
            cop("gpsimd", lambda e: e.memset(VS[:].rearrange("p k h c -> p (k h) c")[:, :, 128:129], 1.0))
            W12 = sb(sa, "W12", [128, KC, 1024], BF16)
            tW12 = T(None)
            xt = [sb(sa, f"xt{i}", [128, D], F32) for i in range(2)]
            txt = [T(None), T(None)]
            stt = [sb(sa, f"st{i}", [128, 8], F32) for i in range(2)]
            tst = [T(None), T(None)]
            xb = sb(sa, "xb", [128, D], BF16)
            txb = T(None)
            xT = sb(sa, "xT", [128, KC, 128], BF16)
            txT = T(None)
            sq = sb(sa, "sq", [128, 512], F32)
            tsq = T(None)
            tmp = sb(sa, "tmp", [128, 512], F32)
            ttmp = T(None)
            f8 = sb(sa, "f8", [128, 32], F32)
            tf8 = T(None)
            knb = sb(sa, "knb", [128, 512], BF16)
            tknb = T(None)

            load_w(W12, tW12, [(C_KB, 512), (C_VB, 512)])
            xTs = [xT, sb(sa, "xT1", [128, KC, 128], BF16)]
            txTs = [txT, T(None)]
            knbs = [knb, sb(sa, "knb1", [128, 512], BF16)]
            tknbs = [tknb, T(None)]

            def p1_s0(kb):
                b2 = kb % 2
                xproc(xkv[kb * 128:(kb + 1) * 128, :], xt[b2], txt[b2], f"dx{b2}", stt[b2], tst[b2], xb, txb,
                      sq[:].bitcast(BF16), tsq, xTs[b2], txTs[b2], bank=4 * b2)

            def p1_s1(kb):
                b2 = kb % 2
                bk = 4 * b2
                proj(xTs[b2], txTs[b2], W12, tW12, 0, 512, bank=bk + 1)
                proj(xTs[b2], txTs[b2], W12, tW12, 512, 512, bank=bk + 2)
                qknorm(bk + 1, 0, 8, 3, stt[b2], tst[b2], sq, tsq, tmp, ttmp, f8, tf8, knbs[b2], tknbs[b2])
                P.op("scalar", lambda e, kb=kb, b2=b2, bk=bk: e.activation(
                    out=VS[:, kb, :, 0:128], in_=PSf[bk + 2].rearrange("p (h c) -> p h c", c=128), func=AF.Copy,
                    scale=stt[b2][:, 3:4]), r=[PS[bk + 2], tst[b2]], w=[tVS[kb]])

            def p1_s2(kb):
                b2 = kb % 2
                bk = 4 * b2
                pv = psb(bk + 3)
                P.grp("tensor", [lambda e, h=h, pv=pv, b2=b2: e.transpose(out=pv[:, h * 128:(h + 1) * 128],
                                                                         in_=knbs[b2][:, h * 128:(h + 1) * 128],
                                                                         identity=ident[:])
                                 for h in range(4)], r=[tknbs[b2], tC], w=[PS[bk + 3]])
                P.op("vector", lambda e, kb=kb, pv=pv: e.tensor_copy(out=KT[:, kb, :, :].rearrange("p h k -> p (h k)"),
                                                                    in_=pv[:, 0:512]), r=[PS[bk + 3]], w=[tKT[kb]])

            NB1 = NB if nphase >= 1 else 0
            for j in range(NB1 + 2):
                if j < NB1:
                    p1_s0(j)
                if 0 <= j - 1 < NB1:
                    p1_s1(j - 1)
                if 0 <= j - 2 < NB1:
                    p1_s2(j - 2)

            load_w(W12, tW12, [(C_QB, 512), (C_ZB, 512)])
            QT = sb(sa, "QT", [128, 4, 256], BF16)
            tQT = T(None)
            PT = [sb(sa, f"PT{i}", [128, 512], BF16) for i in range(3)]
            tPT = [T(None) for _ in range(3)]
            szb = [sb(sa, f"szb{i}", [128, 512], F32) for i in range(2)]
            tszb = [T(None), T(None)]
            osb = sb(sa, "osb", [128, 4, 129], F32)
            tosb = T(None)
            od = sb(sa, "od", [128, 2, 128], F32)
            tod = T(None)
            sm = sb(sa, "sm", [128, 16], F32)
            tsm = T(None)
            yt = sb(sa, "yt", [128, 128], F32)
            tyt = T(None)
            AB = [[4, 5], [4, 5]]
            QTs = [QT, sb(sa, "QT1", [128, 4, 256], BF16)]
            tQTs = [tQT, T(None)]
            hcount = 0
            NCH2 = NCH if nphase >= 2 else 0

            def preamble(i):
                cp = i % 2
                for t in range(2):
                    s_ = 2 * i + t
                    b2 = s_ % 2
                    xT_, txT_, knb_, tknb_ = xTs[t], txTs[t], knbs[t], tknbs[t]
                    xproc(xq[s_ * 128:(s_ + 1) * 128, :], xt[b2], txt[b2], f"dx{b2}", stt[b2], tst[b2], xb, txb,
                          sq[:].bitcast(BF16), tsq, xT_, txT_, bank=6)
                    yield
                    proj(xT_, txT_, W12, tW12, 0, 512, bank=7)
                    yield
                    qknorm(7, 0, 8, 2, stt[b2], tst[b2], sq, tsq, tmp, ttmp, f8, tf8, knb_, tknb_)
                    P.op("vector", lambda e, b2=b2, t=t: e.tensor_copy(out=zst[:, 2 * t:2 * t + 2], in_=stt[b2][:, 3:5]),
                         r=[tst[b2]], w=[tzst])
                    yield
                    pv = psb(7)
                    P.grp("tensor", [lambda e, h=h, pv=pv, knb_=knb_: e.transpose(out=pv[:, h * 128:(h + 1) * 128],
                                                                                 in_=knb_[:, h * 128:(h + 1) * 128],
                                                                                 identity=ident[:])
                                     for h in range(4)], r=[tknb_, tC], w=[PS[7]])
                    P.op("vector", lambda e, t=t, pv=pv, cp=cp: e.tensor_copy(
                        out=QTs[cp][:, :, t * 128:(t + 1) * 128], in_=pv[:, 0:512].rearrange("p (h k) -> p h k", k=128)),
                        r=[PS[7]], w=[tQTs[cp]])
                    yield

            zst = sb(sa, "zst", [128, 4], F32)
            tzst = T(None)

            def preamble_z(i):
                for t in range(2):
                    xT_, txT_ = xTs[t], txTs[t]
                    proj(xT_, txT_, W12, tW12, 512, 512, bank=6)
                    P.op("scalar", lambda e, t=t: e.activation(out=sq[:], in_=PSf[6], func=AF.Exp,
                                                              scale=zst[:, 2 * t + 1:2 * t + 2]), r=[PS[6], tzst], w=[tsq])
                    P.op("scalar", lambda e: e.activation(out=sq[:], in_=sq[:], func=AF.Ln, bias=1.0), r=[tsq], w=[tsq])
                    P.op("scalar", lambda e: e.activation(out=sq[:], in_=sq[:], func=AF.Exp, scale=-1.0), r=[tsq], w=[tsq])
                    P.op("vector", lambda e, t=t: e.tensor_scalar(out=tmp[:], in0=PSf[6], scalar1=zst[:, 2 * t:2 * t + 1],
                                                                 scalar2=None, op0=ALU.mult),
                         r=[PS[6], tzst], w=[ttmp])
                    P.op("vector", lambda e, t=t: e.tensor_tensor(out=szb[t][:], in0=tmp[:], in1=sq[:], op=ALU.mult),
                         r=[ttmp, tsq], w=[tszb[t]])

            def advance(gen, n=1):
                if gen is None:
                    return None
                for _ in range(n):
                    try:
                        next(gen)
                    except StopIteration:
                        return None
                return gen

            def epilogue(h, i, szb, tszb):
                P.op("vector", lambda e: e.reciprocal(out=sm[:, 0:4], in_=osb[:, :, 128]), r=[tosb], w=[tsm])
                P.op("vector", lambda e: e.tensor_scalar(out=sm[:, 2:4], in0=sm[:, 2:4], scalar1=lams[:, 4:5],
                                                        scalar2=None, op0=ALU.mult), r=[tsm, tC], w=[tsm])
                yield
                for t in range(2):
                    P.op("vector", lambda e, t=t: e.tensor_scalar(out=yt[:], in0=osb[:, 2 + t, 0:128],
                                                                 scalar1=sm[:, 2 + t:3 + t], scalar2=None, op0=ALU.mult),
                         r=[tosb, tsm], w=[tyt])
                    P.op("vector", lambda e, t=t: e.scalar_tensor_tensor(
                        out=od[:, t, :], in0=osb[:, t, 0:128], scalar=sm[:, t:t + 1], in1=yt[:],
                        op0=ALU.mult, op1=ALU.add), r=[tosb, tsm, tyt], w=[tod])
                    yield
                    P.op("scalar", lambda e, t=t: e.activation(out=yt[:], in_=od[:, t, :], func=AF.Square,
                                                              accum_out=sm[:, 4 + t:5 + t]), r=[tod], w=[tyt, tsm])
                    yield
                P.op("vector", lambda e: e.tensor_scalar(out=sm[:, 6:8], in0=sm[:, 4:6], scalar1=1.0 / 128, scalar2=EPS,
                                                        op0=ALU.mult, op1=ALU.add), r=[tsm], w=[tsm])
                yield
                P.op("scalar", lambda e: e.activation(out=sm[:, 8:10], in_=sm[:, 6:8], func=AF.Ln), r=[tsm], w=[tsm])
                P.op("scalar", lambda e: e.activation(out=sm[:, 10:12], in_=sm[:, 8:10], func=AF.Exp, scale=-0.5),
                     r=[tsm], w=[tsm])
                yield
                for t in range(2):
                    s_ = 2 * i + t
                    P.op("vector", lambda e, t=t: e.scalar_tensor_tensor(
                        out=yt[:], in0=od[:, t, :], scalar=sm[:, 10 + t:11 + t], in1=subg[:],
                        op0=ALU.mult, op1=ALU.mult), r=[tod, tsm, tC], w=[tyt])
                    P.op("vector", lambda e, t=t, s_=s_, h=h, szb=szb: e.tensor_tensor(
                        out=YB[:, s_, h * 128:(h + 1) * 128], in0=yt[:], in1=szb[t][:, h * 128:(h + 1) * 128],
                        op=ALU.mult), r=[tyt, tszb[t]], w=[tYB[s_]])
                    yield

            epi = None
            pre = preamble(0) if NCH2 > 0 else None
            while pre is not None:
                pre = advance(pre)
            for i in range(NCH2):
                nkb = 9 + 8 * i
                cp = i % 2
                QT, tQT = QTs[cp], tQTs[cp]
                preamble_z(i)
                pre = preamble(i + 1) if i + 1 < NCH2 else None
                every = max(1, (4 * nkb) // 10)
                gstep = 0
                for h in range(4):
                    ab = AB[hcount % 2]
                    hcount += 1
                    first_in_bank = {ab[0]: True, ab[1]: True}

                    def emit_S(kb, ss):
                        P.grp("tensor", [lambda e, m=m, kb=kb, ss=ss, h=h, QT=QT: e.matmul(
                            PP[ss][:, m, 0:256], lhsT=KT[m * 64:(m + 1) * 64, kb, h, :],
                            rhs=QT[m * 64:(m + 1) * 64, h, :], start=True, stop=True, skip_group_check=True)
                            for m in range(2)], r=[tKT[kb], tQT], w=[PS[2 * ss], PS[2 * ss + 1]])

                    def emit_E(kb, ss, slot):
                        if kb == 0:
                            bias = biasB0[:, h, i:i + 1]
                        else:
                            j = (2 + 8 * i - kb) + 6
                            bias = biasB[:, h, j:j + 1]
                        P.op("scalar", lambda e, ss=ss, slot=slot, bias=bias: e.activation(
                            out=PT[slot][:].rearrange("p (m c) -> p m c", m=2), in_=PP[ss][:, :, 0:256], func=AF.Exp,
                            bias=bias, scale=0.125),
                            r=[PS[2 * ss], PS[2 * ss + 1], tC], w=[tPT[slot]])
                        u = kb - 8 * i
                        if u >= 1:
                            for t in range(2):
                                dd = (t - (u - 1)) + 7
                                P.op("vector", lambda e, slot=slot, t=t, dd=dd: e.tensor_tensor(
                                    out=PT[slot][:].rearrange("p (m t q) -> p m t q", m=2, t=2)[:, :, t, :],
                                    in0=PT[slot][:].rearrange("p (m t q) -> p m t q", m=2, t=2)[:, :, t, :],
                                    in1=MASK[:, dd, :].unsqueeze(1).to_broadcast([128, 2, 128]), op=ALU.mult),
                                    r=[tC], w=[tPT[slot]])

                    def emit_AV(kb, slot):
                        fns = []
                        for m in range(2):
                            for t in range(2):
                                bank = ab[m]
                                st_flag = first_in_bank[bank]
                                first_in_bank[bank] = False
                                fns.append(lambda e, m=m, t=t, bank=bank, st_flag=st_flag, kb=kb, slot=slot, h=h: e.matmul(
                                    PSf[bank][:, t * 129:(t + 1) * 129],
                                    lhsT=PT[slot][:, (m * 2 + t) * 128:(m * 2 + t + 1) * 128],
                                    rhs=VS[:, kb, h, :], start=st_flag, stop=(kb == nkb - 1), skip_group_check=True))
                        P.grp("tensor", fns, r=[tPT[slot], tVS[kb]], w=[PS[ab[0]], PS[ab[1]]])

                    emit_S(0, 0)
                    for kb in range(nkb):
                        if kb + 1 < nkb:
                            emit_S(kb + 1, (kb + 1) % 2)
                        emit_E(kb, kb % 2, kb % 3)
                        emit_AV(kb, kb % 3)
                        gstep += 1
                        if gstep % every == 0:
                            pre = advance(pre)
                        if kb >= 1:
                            epi = advance(epi)
                    while epi is not None:
                        epi = advance(epi)
                    for m in range(2):
                        P.op("scalar", lambda e, m=m, ab=ab: e.activation(
                            out=osb[:, 2 * m:2 * m + 2, :].rearrange("p a c -> p (a c)"), in_=PSf[ab[m]][:, 0:258],
                            func=AF.Copy), r=[PS[ab[m]]], w=[tosb])
                    epi = epilogue(h, i, szb, tszb)
                while epi is not None:
                    epi = advance(epi)
                while pre is not None:
                    pre = advance(pre)

            if debug:
                dsc = sb(sa, "dsc", [128, 516], F32)
                tds = T(None)
                for kb in range(NB):
                    P.op("vector", lambda e, kb=kb: e.tensor_copy(out=dsc[:, 0:512], in_=KT[:, kb, :, :].rearrange("p h k -> p (h k)")),
                         r=[tKT[kb]], w=[tds])
                    P.op("sync", lambda e, kb=kb: e.dma_start(out=dbg["d_kt"][:, kb * 512:(kb + 1) * 512], in_=dsc[:, 0:512]),
                         r=[tds], dsem="dd")
                    P.op("vector", lambda e, kb=kb: e.tensor_copy(out=dsc[:, 0:516], in_=VS[:, kb, :, :].rearrange("p h k -> p (h k)")),
                         r=[tVS[kb]], w=[tds])
                    P.op("sync", lambda e, kb=kb: e.dma_start(out=dbg["d_v"][:, kb * 516:(kb + 1) * 516], in_=dsc[:, 0:516]),
                         r=[tds], dsem="dd")
                for s in range(NOWN):
                    P.op("vector", lambda e, s=s: e.tensor_copy(out=dsc[:, 0:512], in_=YB[:, s, :]), r=[tYB[s]], w=[tds])
                    P.op("sync", lambda e, s=s: e.dma_start(out=dbg["d_yb"][:, s * 512:(s + 1) * 512], in_=dsc[:, 0:512]),
                         r=[tds], dsem="dd")
                P.op("sync", lambda e: e.nop(), r=[tds], sig=False)
            P.run_block()

        with ExitStack() as sc:
            NW3 = 512 + 256 + 512 + 2048
            W3 = sb(sc, "W3", [128, KC, NW3], BF16)
            WUA = sb(sc, "WUA", [128, 4, D], BF16)
            WUB = sb(sc, "WUB", [128, 4, D], BF16)
            WO = sb(sc, "WO", [128, KC, D], BF16)
            tW3 = T(None)
            O_QA, O_KV, O_ZA, O_GL = 0, 512, 768, 1280
            tW3b, tWU, tWO = T(None), T(None), T(None)
            def load_w3a():
                load_w(W3, tW3, [(C_QA, 512), (C_KA, 256)], dsem="dw")

            def load_w3b():
                load_w(W3, tW3b, [(C_ZA, 512), (C_GL, 2048)], dsem="dw2", o0=768)

            def load_wuo():
                P.op("gpsimd", lambda e: e.dma_start(out=WUA[:], in_=w_up_a.rearrange("(kc p) c -> p kc c", p=128)), w=[tWU], dsem="dw3")
                P.op("gpsimd", lambda e: e.dma_start(out=WUB[:], in_=w_up_b.rearrange("(kc p) c -> p kc c", p=128)), w=[tWU], dsem="dw3")
                P.op("gpsimd", lambda e: e.dma_start(out=WO[:], in_=w_o.rearrange("(kc p) c -> p kc c", p=128)), w=[tWO], dsem="dw4")

            xt = [sb(sc, f"xt3_{i}", [128, D], F32) for i in range(2)]
            txt = [T(None), T(None)]
            xpt = [sb(sc, f"xpt{i}", [128, D], F32) for i in range(2)]
            txpt = [T(None), T(None)]
            stt = [sb(sc, f"st3_{i}", [128, 8], F32) for i in range(2)]
            tst = [T(None), T(None)]
            stp = [sb(sc, f"stp{i}", [128, 8], F32) for i in range(2)]
            tstp = [T(None), T(None)]
            xb = sb(sc, "xb3", [128, D], BF16)
            txb = T(None)
            xT = sb(sc, "xT3", [128, KC, 128], BF16)
            txT = T(None)
            xTp = sb(sc, "xTp", [128, KC, 128], BF16)
            txTp = T(None)
            sq = sb(sc, "sq3", [128, 512], F32)
            tsq = T(None)
            junk = sq[:].bitcast(BF16)
            tmp = sb(sc, "tmp3", [128, 512], F32)
            ttmp = T(None)
            f8 = sb(sc, "f83", [128, 32], F32)
            tf8 = T(None)
            qab = sb(sc, "qab", [128, 512], BF16)
            tqab = T(None)
            QaT = sb(sc, "QaT", [128, 4, 128], BF16)
            tQaT = T(None)
            kdup = sb(sc, "kdup", [128, 2, 2, 2, 64], BF16)
            tkdup = T(None)
            kab = sb(sc, "kab", [128, 128], BF16)
            tkab = T(None)
            KaT = sb(sc, "KaT", [128, 2, 2, 128], BF16)
            tKaT = T(None)
            Va = sb(sc, "Va", [128, 2, 2, 65], BF16)
            tVa = T(None)
            PTa = sb(sc, "PTa", [128, 2, 8, 128], BF16)
            tPTa = T(None)
            oa = sb(sc, "oa", [128, 8, 65], F32)
            toa = T(None)
            sm = sb(sc, "sm3", [128, 16], F32)
            tsm = T(None)
            ya = sb(sc, "ya", [128, 512], F32)
            tya = T(None)
            e1 = sb(sc, "e1", [128, 512], F32)
            te1 = T(None)
            e2 = sb(sc, "e2", [128, 512], F32)
            te2 = T(None)
            yab = sb(sc, "yab", [128, 512], BF16)
            tyab = T(None)
            yT = sb(sc, "yT", [128, KC, 128], BF16)
            tyT = T(None)
            mixb = sb(sc, "mixb", [128, D], BF16)
            tmixb = T(None)
            mixT = sb(sc, "mixT", [128, KC, 128], BF16)
            tmixT = T(None)
            ot = [sb(sc, f"ot{i}", [128, D], F32) for i in range(2)]
            tot = [T(None), T(None)]
            P.op("gpsimd", lambda e: e.memset(Va[:, :, :, 64:65].rearrange("p a b c -> p (a b) c"), 1.0), w=[tVa])

            xTs3 = [xT, sb(sc, "xT3b", [128, KC, 128], BF16), sb(sc, "xT3c", [128, KC, 128], BF16)]
            txTs3 = [txT, T(None), T(None)]
            xt = xt + [sb(sc, "xt3_2", [128, D], F32), sb(sc, "xt3_3", [128, D], F32)]
            txt = txt + [T(None), T(None)]
            xbo = [xb, sb(sc, "xbo1", [128, D], BF16)]
            txbo = [txb, T(None)]
            xbp = [sb(sc, "xbp0", [128, D], BF16), sb(sc, "xbp1", [128, D], BF16)]
            txbp = [T(None), T(None)]
            stt = stt + [sb(sc, "st3_2", [128, 8], F32), sb(sc, "st3_3", [128, 8], F32)]
            tst = tst + [T(None), T(None)]
            stp = stp + [sb(sc, "stp2", [128, 8], F32)]
            tstp = tstp + [T(None)]
            xTps = [xTp, sb(sc, "xTpb", [128, KC, 128], BF16)]
            txTps = [txTp, T(None)]
            yas = [ya, sb(sc, "ya_b", [128, 512], F32)]
            tyas = [tya, T(None)]
            junk3t = sb(sc, "junk3", [128, D], BF16)
            junk3 = junk3t[:]
            tjunk3 = T(None)

            def p3_load(s):
                b2 = s % 2
                b3 = s % 3
                b4 = s % 4
                xproc(xq[s * 128:(s + 1) * 128, :], xt[b4], txt[b4], f"dx{b4}", stt[b4], tst[b4], xbo[b2], txbo[b2],
                      junk3, tjunk3, None, None, bank=0)
                xproc(xp[s * 128:(s + 1) * 128, :], xpt[b2], txpt[b2], f"dp{b2}", stp[b3], tstp[b3], xbp[b2], txbp[b2],
                      junk3, tjunk3, None, None, bank=0)

            def p3_s0a(s):
                xproc_pe(xbo[s % 2], txbo[s % 2], xTs3[s % 3], txTs3[s % 3], 0)

            def p3_s0b(s):
                xproc_pe(xbp[s % 2], txbp[s % 2], xTps[s % 2], txTps[s % 2], 0)

            def sigm(bank, b3, dst, tdst):
                P.op("scalar", lambda e: e.activation(out=dst[:], in_=PSf[bank], func=AF.Exp, scale=stt[b3][:, 4:5]),
                     r=[PS[bank], tst[b3]], w=[tdst])
                P.op("scalar", lambda e: e.activation(out=dst[:], in_=dst[:], func=AF.Ln, bias=1.0), r=[tdst], w=[tdst])
                P.op("scalar", lambda e: e.activation(out=dst[:], in_=dst[:], func=AF.Exp, scale=-1.0), r=[tdst], w=[tdst])

            def s1_chunks(s):
                b2 = s % 2
                b3 = s % 3
                xT_, txT_, xTp_, txTp_ = xTs3[b3], txTs3[b3], xTps[b2], txTps[b2]

                def B1():
                    proj(xT_, txT_, W3, tW3, O_QA, 512, bank=1)
                    proj(xT_, txT_, W3, tW3, O_KV, 256, bank=2, boff=256, first=True)
                    proj(xTp_, txTp_, W3, tW3, O_KV, 256, bank=2, boff=0, first=False)

                def B2q():
                    qknorm(1, 0, 8, 0, stt[s % 4], tst[s % 4], sq, tsq, tmp, ttmp, f8, tf8, qab, tqab)

                def B3():
                    pv = psb(0)
                    P.grp("tensor", [lambda e, j=j, pv=pv: e.transpose(out=pv[:, j * 128:(j + 1) * 128],
                                                                      in_=qab[:, j * 128:(j + 1) * 128], identity=ident[:])
                                     for j in range(4)], r=[tqab, tC], w=[PS[0]])
                    P.op("vector", lambda e, pv=pv: e.tensor_copy(out=QaT[:].rearrange("p j k -> p (j k)"), in_=pv[:, 0:512]),
                         r=[PS[0]], w=[tQaT])

                def B2k():
                    for kblk in range(2):
                        stx, tstx = (stp[b3], tstp[b3]) if kblk == 0 else (stt[s % 4], tst[s % 4])
                        boff = 0 if kblk == 0 else 256
                        qknorm(2, boff, 2, 1, stx, tstx, sq, tsq, tmp, ttmp, f8, tf8, kab, tkab)
                        for dup in range(2):
                            P.op("gpsimd", lambda e, kblk=kblk, dup=dup: e.tensor_copy(
                                out=kdup[:, kblk, :, dup, :], in_=kab[:, 0:128].rearrange("p (h d) -> p h d", d=64)),
                                r=[tkab], w=[tkdup])
                        P.op("scalar", lambda e, kblk=kblk, boff=boff, stx=stx: e.activation(
                            out=Va[:, kblk, :, 0:64],
                            in_=PSf[2][:, boff + 128:boff + 256].rearrange("p (h d) -> p h d", d=64),
                            func=AF.Copy, scale=stx[:, 3:4]), r=[PS[2], tstx], w=[tVa])

                def B4():
                    pv = psb(0)
                    P.grp("tensor", [lambda e, kblk=kblk, kvh=kvh, pv=pv: e.transpose(
                        out=pv[:, (kblk * 2 + kvh) * 128:(kblk * 2 + kvh + 1) * 128],
                        in_=kdup[:, kblk, kvh, :, :].rearrange("p a d -> p (a d)"), identity=ident[:])
                        for kblk in range(2) for kvh in range(2)], r=[tkdup, tC], w=[PS[0]])
                    P.op("vector", lambda e, pv=pv: e.tensor_copy(out=KaT[:].rearrange("p a b k -> p (a b k)"),
                                                                 in_=pv[:, 0:512]), r=[PS[0]], w=[tKaT])

                def B5():
                    for kblk in range(2):
                        sbk = [4, 5] if kblk == 0 else [1, 2]
                        fns = []
                        for j in range(4):
                            kvh = j // 2
                            for ee in range(2):
                                fns.append(lambda e, kblk=kblk, kvh=kvh, j=j, ee=ee, sbk=sbk: e.matmul(
                                    PSf[sbk[ee]][:, j * 128:(j + 1) * 128],
                                    lhsT=KaT[ee * 64:(ee + 1) * 64, kblk, kvh, :],
                                    rhs=QaT[ee * 64:(ee + 1) * 64, j, :], start=True, stop=True, skip_group_check=True))
                        P.grp("tensor", fns, r=[tKaT, tQaT], w=[PS[sbk[0]], PS[sbk[1]]])
                        for j in range(4):
                            for ee in range(2):
                                hq = 2 * j + ee
                                P.op("scalar", lambda e, kblk=kblk, hq=hq, j=j, ee=ee, sbk=sbk: e.activation(
                                    out=PTa[:, kblk, hq, :], in_=PSf[sbk[ee]][:, j * 128:(j + 1) * 128], func=AF.Exp,
                                    bias=biasA[:, hq, kblk:kblk + 1], scale=0.125),
                                    r=[PS[sbk[ee]], tC], w=[tPTa])
                    P.op("vector", lambda e, s=s: e.scalar_tensor_tensor(
                        out=PTa[:, 0, :, :], in0=PTa[:, 0, :, :], scalar=kvalA[:, s:s + 1],
                        in1=Mprev[:].unsqueeze(1).to_broadcast([128, 8, 128]), op0=ALU.mult, op1=ALU.mult),
                        r=[tC], w=[tPTa])
                    P.op("vector", lambda e: e.tensor_tensor(
                        out=PTa[:, 1, :, :], in0=PTa[:, 1, :, :], in1=Mown[:].unsqueeze(1).to_broadcast([128, 8, 128]),
                        op=ALU.mult), r=[tC], w=[tPTa])

                def B6():
                    fib = {1: True, 2: True}
                    fns = []
                    for hq in range(8):
                        bank = 1 + hq // 4
                        hl = hq % 4
                        kvh = hq // 4
                        for kblk in range(2):
                            sf = fib[bank]
                            fib[bank] = False
                            fns.append(lambda e, hq=hq, hl=hl, kvh=kvh, kblk=kblk, bank=bank, sf=sf: e.matmul(
                                PSf[bank][:, hl * 65:(hl + 1) * 65], lhsT=PTa[:, kblk, hq, :], rhs=Va[:, kblk, kvh, :],
                                start=sf, stop=(kblk == 1), skip_group_check=True))
                    P.grp("tensor", fns, r=[tPTa, tVa], w=[PS[1], PS[2]])
                    for bnk in range(2):
                        P.op("scalar", lambda e, bnk=bnk: e.activation(
                            out=oa[:, bnk * 4:(bnk + 1) * 4, :].rearrange("p a c -> p (a c)"), in_=PSf[1 + bnk][:, 0:260],
                            func=AF.Copy), r=[PS[1 + bnk]], w=[toa])
                    P.op("vector", lambda e: e.tensor_tensor(out=sm[:, 0:8], in0=oa[:, :, 64], in1=sinkexp[:], op=ALU.add),
                         r=[toa, tC], w=[tsm])
                    P.op("vector", lambda e: e.reciprocal(out=sm[:, 8:16], in_=sm[:, 0:8]), r=[tsm], w=[tsm])
                    P.op("vector", lambda e: e.tensor_tensor(
                        out=yas[b2][:].rearrange("p (h d) -> p h d", d=64), in0=oa[:, :, 0:64],
                        in1=sm[:, 8:16].unsqueeze(2).to_broadcast([128, 8, 64]), op=ALU.mult),
                        r=[toa, tsm], w=[tyas[b2]])

                return dict(B1=B1, B2q=B2q, B2k=B2k, B3=B3, B4=B4, B5=B5, B6=B6)

            def s2_chunks(s):
                b2 = s % 2
                b3 = s % 3
                xT_, txT_ = xTs3[b3], txTs3[b3]

                def C1():
                    proj(xT_, txT_, W3, tW3b, O_ZA, 512, bank=6)
                    sigm(6, s % 4, e1, te1)
                    P.op("vector", lambda e: e.scalar_tensor_tensor(
                        out=e2[:], in0=PSf[6], scalar=stt[s % 4][:, 3:4], in1=yas[b2][:], op0=ALU.mult, op1=ALU.mult),
                        r=[PS[6], tst[s % 4], tyas[b2]], w=[te2])
                    P.op("vector", lambda e: e.tensor_tensor(out=yab[:], in0=e2[:], in1=e1[:], op=ALU.mult),
                         r=[te2, te1], w=[tyab])

                def C2():
                    pv = psb(6)
                    fns = []
                    for k in range(8):
                        src = yab[:, k * 128:(k + 1) * 128] if k < 4 else YB[:, s, (k - 4) * 128:(k - 3) * 128]
                        fns.append(lambda e, k=k, src=src, pv=pv: e.transpose(out=pv[:, k * 128:(k + 1) * 128], in_=src,
                                                                             identity=ident[:]))
                    P.grp("tensor", fns, r=[tyab, tYB[s], tC], w=[PS[6]])
                    P.op("vector", lambda e, pv=pv: e.tensor_copy(out=yT[:].rearrange("p k t -> p (k t)"),
                                                                 in_=pv[:, 0:1024]), r=[PS[6]], w=[tyT])

                def C3(g):
                    proj(xT_, txT_, W3, tW3b, O_GL + g * 512, 512, bank=7)
                    proj(xT_, txT_, W3, tW3b, O_GL + 1024 + g * 512, 512, bank=3)
                    sigm(7, s % 4, sgA[g], tsgA[g])
                    sigm(3, s % 4, sgB[g], tsgB[g])

                def C4a(g):
                    P.grp("tensor", [lambda e, k=k, g=g: e.matmul(PSf[6], lhsT=yT[:, k, :],
                                                                 rhs=WUA[:, k, g * 512:(g + 1) * 512],
                                                                 start=(k == 0), stop=(k == 3), skip_group_check=True)
                                     for k in range(4)], r=[tyT, tWU], w=[PS[6]])
                    P.op("vector", lambda e, g=g: e.tensor_tensor(out=sgA[g][:], in0=PSf[6], in1=sgA[g][:], op=ALU.mult),
                         r=[PS[6], tsgA[g]], w=[tsgA[g]])

                def C4b(g):
                    P.grp("tensor", [lambda e, k=k, g=g: e.matmul(PSf[0], lhsT=yT[:, 4 + k, :],
                                                                 rhs=WUB[:, k, g * 512:(g + 1) * 512],
                                                                 start=(k == 0), stop=(k == 3), skip_group_check=True)
                                     for k in range(4)], r=[tyT, tWU], w=[PS[0]])
                    P.op("vector", lambda e, g=g: e.tensor_tensor(out=sgB[g][:], in0=PSf[0], in1=sgB[g][:], op=ALU.mult),
                         r=[PS[0], tsgB[g]], w=[tsgB[g]])
                    P.op("vector", lambda e, g=g: e.tensor_tensor(out=mixb[:, g * 512:(g + 1) * 512], in0=sgA[g][:],
                                                                 in1=sgB[g][:], op=ALU.add),
                         r=[tsgA[g], tsgB[g]], w=[tmixb])

                def C5():
                    pv = psb(6)
                    P.grp("tensor", [lambda e, k=k, pv=pv: e.transpose(out=pv[:, k * 128:(k + 1) * 128],
                                                                      in_=mixb[:, k * 128:(k + 1) * 128], identity=ident[:])
                                     for k in range(8)], r=[tmixb, tC], w=[PS[6]])
                    P.op("vector", lambda e, pv=pv: e.tensor_copy(out=mixT[:].rearrange("p k t -> p (k t)"),
                                                                 in_=pv[:, 0:1024]), r=[PS[6]], w=[tmixT])
                    for g in range(2):
                        bank = 3 if g == 0 else 7
                        P.grp("tensor", [lambda e, k=k, g=g, bank=bank: e.matmul(
                            PSf[bank], lhsT=mixT[:, k, :], rhs=WO[:, k, g * 512:(g + 1) * 512],
                            start=(k == 0), stop=(k == 7), skip_group_check=True) for k in range(8)],
                            r=[tmixT, tWO], w=[PS[bank]])
                        P.op("vector", lambda e, g=g, bank=bank: e.tensor_tensor(
                            out=ot[b2][:, g * 512:(g + 1) * 512], in0=PSf[bank], in1=xt[s % 4][:, g * 512:(g + 1) * 512],
                            op=ALU.add), r=[PS[bank], txt[s % 4]], w=[tot[b2]])
                    P.op("sync", lambda e: e.dma_start(out=out[s * 128:(s + 1) * 128, :], in_=ot[b2][:]),
                         r=[tot[b2]], dsem=f"do{b2}")

                return dict(C1=C1, C2=C2, C3=C3, C4a=C4a, C4b=C4b, C5=C5)

            sgA = [e1, sb(sc, "sgA1", [128, 512], F32)]
            tsgA = [te1, T(None)]
            sgB = [sb(sc, "sgB0", [128, 512], F32), sb(sc, "sgB1", [128, 512], F32)]
            tsgB = [T(None), T(None)]

            N3 = NOWN if nphase >= 3 else 0
            for j in range(N3 + 2):
                A = j if j < N3 else None
                B = s1_chunks(j - 1) if 0 <= j - 1 < N3 else None
                C = s2_chunks(j - 2) if 0 <= j - 2 < N3 else None

                def run(d, name, *args):
                    if d is not None:
                        d[name](*args)

                if j == 0:
                    if N3 > 0:
                        p3_load(0)
                    load_w3a()
                    if N3 > 0:
                        p3_s0a(0)
                        p3_s0b(0)
                run(C, "C1")
                run(B, "B1")
                run(B, "B2q")
                run(B, "B2k")
                if j + 1 < N3:
                    p3_load(j + 1)
                if j == 0:
                    load_w3b()
                if j == 1 or (j == 0 and N3 < 2):
                    load_wuo()
                run(B, "B3")
                run(C, "C3", 0)
                run(C, "C2")
                run(B, "B4")
                run(C, "C4a", 0)
                run(B, "B5")
                run(C, "C4b", 0)
                run(C, "C3", 1)
                run(B, "B6")
                run(C, "C4a", 1)
                run(C, "C4b", 1)
                if j + 1 < N3:
                    p3_s0a(j + 1)
                    p3_s0b(j + 1)
                run(C, "C5")
            P.op("sync", lambda e: e.nop(), w=[tot[0], tot[1]], sig=False)
            P.run_block()
    return nc


def make_in_maps(inputs, NCH=8):
    NB = 1 + 8 * NCH
    x = np.asarray(inputs["x"], dtype=np.float32)
    meta = np.asarray(inputs["meta"], dtype=np.float32)
    B = x.shape[0]
    shared = {}
    for k in ["w_in", "w_up_a", "w_up_b", "w_o"]:
        shared[k] = np.ascontiguousarray(np.asarray(inputs[k], dtype=np.float32)[0])
    for k in ["norm_g", "a_qn", "a_kn", "a_sink", "b_qn", "b_kn", "b_lq1", "b_lk1", "b_lq2", "b_lk2", "b_subln"]:
        shared[k] = np.ascontiguousarray(np.asarray(inputs[k], dtype=np.float32).reshape(1, -1))
    in_maps = []
    blocks = []
    for core in range(8):
        b, c = core // 4, core % 4
        h = np.concatenate([np.zeros((NPAD, D), np.float32), meta, x[b]], axis=0)
        assert h.shape[0] == NB * 128
        own = [1 + 2 * (c + 4 * i) + t for i in range(NCH) for t in range(2)]
        xq = np.concatenate([h[n * 128:(n + 1) * 128] for n in own], axis=0)
        xp = np.concatenate([h[(n - 1) * 128:n * 128] for n in own], axis=0)
        cvec = np.zeros((128, 2), np.float32)
        cvec[:, 0] = 2 * c
        cvec[:, 1] = 128 * 2 * c
        m = dict(shared)
        m.update({"xkv": np.ascontiguousarray(h), "xq": np.ascontiguousarray(xq), "xp": np.ascontiguousarray(xp),
                  "cvec": cvec})
        in_maps.append(m)
        blocks.append((b, own))
    return in_maps, blocks


_CACHE = {}


def kernel(**inputs):
    NCH = 8
    x = np.asarray(inputs["x"])
    B, S, _ = x.shape
    in_maps, blocks = make_in_maps(inputs, NCH)
    if "nc" not in _CACHE:
        _CACHE["nc"] = build_program(NCH)
    nc = _CACHE["nc"]
    res = run_bass_kernel_spmd(nc, in_maps, core_ids=list(range(8)))
    outp = np.zeros((B, S, D), np.float32)
    for core in range(8):
        b, own = blocks[core]
        o = res.results[core]["out"]
        for si, n in enumerate(own):
            outp[b, (n - 1) * 128:n * 128, :] = o[si * 128:(si + 1) * 128, :]
    return outp
```

```python
from contextlib import ExitStack

import numpy as np
import concourse.bass as bass
import concourse.mybir as mybir
from concourse.bass_utils import run_bass_kernel_spmd

F32 = mybir.dt.float32
BF16 = mybir.dt.bfloat16
AF = mybir.ActivationFunctionType
ALU = mybir.AluOpType
AX = mybir.AxisListType

D = 1024
KC = 8
EPS = 1e-6
N_META = 16
NPAD = 128 - N_META
LAM_INIT = 0.8 - 0.6 * 1.0
NEGBIG = -30000.0

C_QA, C_KA, C_VA, C_ZA, C_QB, C_KB, C_VB, C_ZB, C_GL = 0, 512, 640, 768, 1280, 1792, 2304, 2816, 3328

ENGS = ["sync", "scalar", "vector", "gpsimd", "tensor"]


class T:
    def __init__(self, ap, excl=False):
        self.ap = ap
        self.w = None
        self.r = {}
        self.excl = excl


class Prog:
    def __init__(self, nc, sems):
        self.nc = nc
        self.sems = sems
        self.cnt = {k: 0 for k in sems}
        self.waited = {}
        self.q = {e: [] for e in ENGS}

    def op(self, eng, fn, r=(), w=(), dsem=None, extra=(), sig=True):
        toks = []
        for t in r:
            if t.w is not None:
                toks.append(t.w)
            if t.excl:
                for n, v in t.r.items():
                    if n != eng:
                        toks.append((n, v))
        for t in w:
            if t.w is not None:
                toks.append(t.w)
            for n, v in t.r.items():
                toks.append((n, v))
        toks.extend([x for x in extra if x is not None])
        ws = {}
        for (n, v) in toks:
            if eng == "tensor" and n == "tensor":
                continue
            if self.waited.get((eng, n), -1) >= v:
                continue
            ws[n] = max(ws.get(n, -1), v)
        for n, v in ws.items():
            self.waited[(eng, n)] = v
        tok = None
        sg = None
        if sig:
            if dsem is not None:
                sg = (dsem, 16)
            else:
                sg = (eng, 1)
            self.cnt[sg[0]] += sg[1]
            tok = (sg[0], self.cnt[sg[0]])
            for t in w:
                t.w = tok
                t.r = {}
            for t in r:
                t.r[tok[0]] = max(t.r.get(tok[0], -1), tok[1])
        sems = self.sems
        wl = list(ws.items())

        def emit(e, fn=fn, wl=wl, sg=sg):
            for (n, v) in wl:
                e.wait_ge(sems[n], v)
            ins = fn(e)
            if sg is not None:
                ins.then_inc(sems[sg[0]], sg[1])
        self.q[eng].append(emit)
        return tok

    def grp(self, eng, fns, r=(), w=()):
        n = len(fns)
        tok = None
        for i, fn in enumerate(fns):
            if n == 1:
                tok = self.op(eng, fn, r=r, w=w)
            elif i == 0:
                self.op(eng, fn, r=r, w=w, sig=False)
            elif i == n - 1:
                tok = self.op(eng, fn, r=r, w=w)
            else:
                self.op(eng, fn, sig=False)
        return tok

    def run_block(self):
        q = self.q
        with self.nc.Block() as block:
            @block.sync
            def _(e):
                for f in q["sync"]:
                    f(e)

            @block.scalar
            def _(e):
                for f in q["scalar"]:
                    f(e)

            @block.vector
            def _(e):
                for f in q["vector"]:
                    f(e)

            @block.gpsimd
            def _(e):
                for f in q["gpsimd"]:
                    f(e)

            @block.tensor
            def _(e):
                for f in q["tensor"]:
                    f(e)
        self.q = {e: [] for e in ENGS}


def build_program(NCH=8, debug=False, nphase=3):
    NB = 1 + 8 * NCH
    NOWN = 2 * NCH
    nc = bass.Bass("TRN2", target_bir_lowering=False)

    def din(name, shape):
        return nc.dram_tensor(name, shape, F32, kind="ExternalInput").ap()

    xkv = din("xkv", [NB * 128, D])
    xq = din("xq", [NOWN * 128, D])
    xp = din("xp", [NOWN * 128, D])
    cvec = din("cvec", [128, 2])
    w_in = din("w_in", [D, 5376])
    w_up_a = din("w_up_a", [512, D])
    w_up_b = din("w_up_b", [512, D])
    w_o = din("w_o", [D, D])
    norm_g = din("norm_g", [1, D])
    a_qn = din("a_qn", [1, 64])
    a_kn = din("a_kn", [1, 64])
    a_sink = din("a_sink", [1, 8])
    b_qn = din("b_qn", [1, 64])
    b_kn = din("b_kn", [1, 64])
    b_lq1 = din("b_lq1", [1, 64])
    b_lk1 = din("b_lk1", [1, 64])
    b_lq2 = din("b_lq2", [1, 64])
    b_lk2 = din("b_lk2", [1, 64])
    b_subln = din("b_subln", [1, 128])
    out = nc.dram_tensor("out", [NOWN * 128, D], F32, kind="ExternalOutput").ap()
    dbg = {}
    if debug:
        dbg["d_kt"] = nc.dram_tensor("d_kt", [128, NB * 512], F32, kind="ExternalOutput").ap()
        dbg["d_v"] = nc.dram_tensor("d_v", [128, NB * 516], F32, kind="ExternalOutput").ap()
        dbg["d_yb"] = nc.dram_tensor("d_yb", [128, NOWN * 512], F32, kind="ExternalOutput").ap()

    sem_names = ["sync", "scalar", "vector", "gpsimd", "tensor",
                 "dx0", "dx1", "dx2", "dx3", "dp0", "dp1", "dw", "dw2", "dw3", "dw4", "dc", "do0", "do1", "dd"]

    with ExitStack() as es:
        sems = {n: es.enter_context(nc.semaphore(n)) for n in sem_names}
        P = Prog(nc, sems)

        def sb(scope, name, shape, dt):
            return scope.enter_context(nc.sbuf_tensor(name, shape, dt))

        YB = sb(es, "YB", [128, max(NOWN, 6), 512], BF16)
        g_rep = sb(es, "g_rep", [128, D], F32)
        ident = sb(es, "ident", [128, 128], BF16)
        cv = sb(es, "cv", [128, 2], F32)
        kio = sb(es, "kio", [128, 1], F32)
        gains = sb(es, "gains", [128, 4, 64], F32)
        lams = sb(es, "lams", [128, 8], F32)
        subg = sb(es, "subg", [128, 128], F32)
        sinkb = sb(es, "sinkb", [128, 8], F32)
        sinkexp = sb(es, "sinkexp", [128, 8], F32)
        NJ = 8 * NCH + 1
        biasB = sb(es, "biasB", [128, 4, NJ], F32)
        biasB0 = sb(es, "biasB0", [128, 4, NCH], F32)
        biasA = sb(es, "biasA", [128, 8, 2], F32)
        MASK = sb(es, "MASK", [128, 9, 128], BF16)
        Mown = sb(es, "Mown", [128, 128], BF16)
        Mprev = sb(es, "Mprev", [128, 128], BF16)
        kvalA = sb(es, "kvalA", [128, NOWN], F32)
        scr = YB[:].rearrange("p a c -> p (a c)").bitcast(F32)
        lamt = scr[:, 1280:1536].rearrange("p (a d) -> p a d", d=64)
        pad0 = sb(es, "pad0", [128, 1], F32)

        PP = [es.enter_context(nc.psum_tensor(f"pp{i}", [128, 2, 512], F32)) for i in range(4)]
        PSf = [PP[i // 2][:, i % 2, :] for i in range(8)]
        PS = [T(None, excl=True) for _ in range(8)]

        def psb(i):
            return PSf[i].bitcast(BF16)

        tKT = [T(None) for _ in range(NB)]
        tVS = [T(None) for _ in range(NB)]
        tYB = [T(None) for _ in range(NOWN)]
        tC = T(None)

        def cload(dst, src):
            P.op("sync", lambda e: e.dma_start(out=dst, in_=src), w=[tC], dsem="dc")

        cload(g_rep[:], norm_g.partition_broadcast(128))
        cload(cv[:], cvec)
        cload(gains[:, 0, :], a_qn.partition_broadcast(128))
        cload(gains[:, 1, :], a_kn.partition_broadcast(128))
        cload(gains[:, 2, :], b_qn.partition_broadcast(128))
        cload(gains[:, 3, :], b_kn.partition_broadcast(128))
        cload(lamt[:, 0, :], b_lq1.partition_broadcast(128))
        cload(lamt[:, 1, :], b_lk1.partition_broadcast(128))
        cload(lamt[:, 2, :], b_lq2.partition_broadcast(128))
        cload(lamt[:, 3, :], b_lk2.partition_broadcast(128))
        cload(subg[:], b_subln.partition_broadcast(128))
        cload(sinkb[:], a_sink.partition_broadcast(128))

        def cop(eng, fn):
            P.op(eng, fn, r=[tC], w=[tC])

        cop("gpsimd", lambda e: e.iota(kio[:], pattern=[[0, 1]], base=0, channel_multiplier=1,
                                      allow_small_or_imprecise_dtypes=True))
        cop("gpsimd", lambda e: e.iota(scr[:, 0:128], pattern=[[1, 128]], base=0, channel_multiplier=-1,
                                      allow_small_or_imprecise_dtypes=True))
        cop("vector", lambda e: e.tensor_scalar(out=ident[:], in0=scr[:, 0:128], scalar1=0.0, scalar2=None,
                                               op0=ALU.is_equal))
        cop("vector", lambda e: e.tensor_scalar(out=Mown[:], in0=scr[:, 0:128], scalar1=0.0, scalar2=None,
                                               op0=ALU.is_ge))
        cop("vector", lambda e: e.tensor_scalar(out=Mprev[:], in0=scr[:, 0:128], scalar1=0.0, scalar2=None,
                                               op0=ALU.is_lt))
        cop("gpsimd", lambda e: e.iota(scr[:, 0:9 * 128].rearrange("p (a q) -> p a q", q=128),
                                      pattern=[[128, 9], [1, 128]], base=-7 * 128, channel_multiplier=-1,
                                      allow_small_or_imprecise_dtypes=True))
        cop("vector", lambda e: e.tensor_scalar(out=MASK[:].rearrange("p a q -> p (a q)"), in0=scr[:, 0:9 * 128],
                                               scalar1=cv[:, 1:2], scalar2=0.0, op0=ALU.add, op1=ALU.is_ge))
        cop("vector", lambda e: e.tensor_scalar(out=pad0[:], in0=kio[:], scalar1=float(NPAD), scalar2=NEGBIG,
                                               op0=ALU.is_lt, op1=ALU.mult))
        cop("gpsimd", lambda e: e.iota(scr[:, 0:NJ], pattern=[[1, NJ]], base=-6, channel_multiplier=0,
                                      allow_small_or_imprecise_dtypes=True))
        cop("vector", lambda e: e.tensor_scalar(out=scr[:, 0:NJ], in0=scr[:, 0:NJ], scalar1=cv[:, 0:1], scalar2=0.0,
                                               op0=ALU.add, op1=ALU.max))
        cop("vector", lambda e: e.tensor_scalar(out=scr[:, 0:NJ], in0=scr[:, 0:NJ], scalar1=-128.0, scalar2=kio[:, 0:1],
                                               op0=ALU.mult, op1=ALU.add))
        for h in range(4):
            sl = 2.0 ** (-2 * (h + 1))
            cop("vector", lambda e, h=h, sl=sl: e.tensor_scalar(out=biasB[:, h, :], in0=scr[:, 0:NJ], scalar1=sl,
                                                               scalar2=None, op0=ALU.mult))
        cop("gpsimd", lambda e: e.iota(scr[:, 256:256 + NCH], pattern=[[8, NCH]], base=2, channel_multiplier=0,
                                      allow_small_or_imprecise_dtypes=True))
        cop("vector", lambda e: e.tensor_scalar(out=scr[:, 256:256 + NCH], in0=scr[:, 256:256 + NCH], scalar1=cv[:, 0:1],
                                               scalar2=-128.0, op0=ALU.add, op1=ALU.mult))
        cop("vector", lambda e: e.tensor_scalar(out=scr[:, 256:256 + NCH], in0=scr[:, 256:256 + NCH], scalar1=kio[:, 0:1],
                                               scalar2=None, op0=ALU.add))
        for h in range(4):
            sl = 2.0 ** (-2 * (h + 1))
            cop("vector", lambda e, h=h, sl=sl: e.tensor_scalar(out=biasB0[:, h, :], in0=scr[:, 256:256 + NCH], scalar1=sl,
                                                               scalar2=pad0[:, 0:1], op0=ALU.mult, op1=ALU.add))
        for hq in range(8):
            sl = 2.0 ** (-(hq + 1))
            for kblk in range(2):
                off = -64.0 - (128.0 if kblk == 0 else 0.0)
                cop("vector", lambda e, hq=hq, kblk=kblk, sl=sl, off=off: e.tensor_scalar(
                    out=biasA[:, hq, kblk:kblk + 1], in0=kio[:], scalar1=off, scalar2=sl, op0=ALU.add, op1=ALU.mult))
        for hq in range(8):
            sl = 2.0 ** (-(hq + 1))
            cop("vector", lambda e, hq=hq, sl=sl: e.tensor_scalar(out=sinkexp[:, hq:hq + 1], in0=kio[:], scalar1=-64.0,
                                                                 scalar2=sl, op0=ALU.add, op1=ALU.mult))
        cop("vector", lambda e: e.tensor_tensor(out=sinkexp[:], in0=sinkexp[:], in1=sinkb[:], op=ALU.add))
        cop("scalar", lambda e: e.activation(out=sinkexp[:], in_=sinkexp[:], func=AF.Exp))
        cop("gpsimd", lambda e: e.iota(scr[:, 512:512 + NOWN].rearrange("p (i t) -> p i t", t=2),
                                      pattern=[[1024, NCH], [128, 2]], base=-NPAD, channel_multiplier=1,
                                      allow_small_or_imprecise_dtypes=True))
        cop("vector", lambda e: e.tensor_scalar(out=kvalA[:], in0=scr[:, 512:512 + NOWN], scalar1=cv[:, 1:2], scalar2=0.0,
                                               op0=ALU.add, op1=ALU.is_ge))
        cop("vector", lambda e: e.tensor_tensor(out=lamt[:, 0, :], in0=lamt[:, 0, :], in1=lamt[:, 1, :], op=ALU.mult))
        cop("vector", lambda e: e.tensor_tensor(out=lamt[:, 2, :], in0=lamt[:, 2, :], in1=lamt[:, 3, :], op=ALU.mult))
        cop("vector", lambda e: e.reduce_sum(out=lams[:, 0:1], in_=lamt[:, 0, :], axis=AX.X))
        cop("vector", lambda e: e.reduce_sum(out=lams[:, 1:2], in_=lamt[:, 2, :], axis=AX.X))
        cop("scalar", lambda e: e.activation(out=lams[:, 2:4], in_=lams[:, 0:2], func=AF.Exp))
        cop("vector", lambda e: e.tensor_tensor(out=lams[:, 4:5], in0=lams[:, 3:4], in1=lams[:, 2:3], op=ALU.subtract))
        cop("vector", lambda e: e.tensor_scalar(out=lams[:, 4:5], in0=lams[:, 4:5], scalar1=-LAM_INIT, scalar2=None,
                                               op0=ALU.add))
        cop("vector", lambda e: e.tensor_scalar(out=subg[:], in0=subg[:], scalar1=1.0 - LAM_INIT, scalar2=None,
                                               op0=ALU.mult))

        def load_w(dst_tile, tw, col_slices, dsem="dw", o0=0):
            o = o0
            for (c0, n) in col_slices:
                src = w_in[:, c0:c0 + n].rearrange("(kc p) c -> p kc c", p=128)
                P.op("gpsimd", lambda e, o=o, n=n, src=src: e.dma_start(out=dst_tile[:, :, o:o + n], in_=src),
                     w=[tw], dsem=dsem)
                o += n

        def xproc(xsrc_ap, xt, txt, dsem, st, tst, xb, txb, junk, tjunk, xT, txT, bank):
            P.op("sync", lambda e: e.dma_start(out=xt[:], in_=xsrc_ap), w=[txt], dsem=dsem)
            P.op("scalar", lambda e: e.activation(out=junk, in_=xt[:], func=AF.Square, accum_out=st[:, 0:1]),
                 r=[txt], w=[tjunk, tst])
            P.op("vector", lambda e: e.tensor_scalar(out=st[:, 1:2], in0=st[:, 0:1], scalar1=1.0 / D, scalar2=EPS,
                                                    op0=ALU.mult, op1=ALU.add), r=[tst], w=[tst])
            P.op("scalar", lambda e: e.activation(out=st[:, 2:3], in_=st[:, 1:2], func=AF.Ln), r=[tst], w=[tst])
            P.op("scalar", lambda e: e.activation(out=st[:, 3:4], in_=st[:, 2:3], func=AF.Exp, scale=-0.5),
                 r=[tst], w=[tst])
            P.op("vector", lambda e: e.tensor_scalar(out=st[:, 4:5], in0=st[:, 3:4], scalar1=-1.0, scalar2=None,
                                                    op0=ALU.mult), r=[tst], w=[tst])
            P.op("vector", lambda e: e.tensor_scalar(out=st[:, 5:6], in0=st[:, 3:4], scalar1=st[:, 3:4], scalar2=1.0 / 64,
                                                    op0=ALU.mult, op1=ALU.mult), r=[tst], w=[tst])
            P.op("vector", lambda e: e.tensor_scalar(out=st[:, 6:7], in0=st[:, 2:3], scalar1=-0.5, scalar2=None,
                                                    op0=ALU.mult), r=[tst], w=[tst])
            P.op("gpsimd", lambda e: e.tensor_tensor(out=xb[:], in0=xt[:], in1=g_rep[:], op=ALU.mult),
                 r=[txt, tC], w=[txb])
            if xT is not None:
                xproc_pe(xb, txb, xT, txT, bank)

        def xproc_pe(xb, txb, xT, txT, bank):
            pv = psb(bank)
            P.grp("tensor", [lambda e, kc=kc: e.transpose(out=pv[:, kc * 128:(kc + 1) * 128],
                                                         in_=xb[:, kc * 128:(kc + 1) * 128], identity=ident[:])
                             for kc in range(KC)], r=[txb, tC], w=[PS[bank]])
            P.op("vector", lambda e: e.tensor_copy(out=xT[:].rearrange("p k t -> p (k t)"), in_=pv[:, 0:1024]),
                 r=[PS[bank]], w=[txT])

        def proj(xT, txT, W, tW, c0, n, bank, boff=0, first=True):
            P.grp("tensor", [lambda e, kc=kc: e.matmul(PSf[bank][:, boff:boff + n], lhsT=xT[:, kc, :],
                                                      rhs=W[:, kc, c0:c0 + n], start=(kc == 0 and first),
                                                      stop=(kc == KC - 1), skip_group_check=True)
                             for kc in range(KC)], r=[txT, tW], w=[PS[bank]])

        def qknorm(bank, boff, G, gain_idx, st, tst, sq, tsq, tmp, ttmp, f8, tf8, outb, toutb):
            n = G * 64
            src = PSf[bank][:, boff:boff + n]
            P.op("scalar", lambda e: e.activation(out=sq[:, 0:n], in_=src, func=AF.Square), r=[PS[bank]], w=[tsq])
            P.op("vector", lambda e: e.reduce_sum(out=f8[:, 0:G], in_=sq[:, 0:n].rearrange("p (g d) -> p g d", d=64),
                                                 axis=AX.X), r=[tsq], w=[tf8])
            P.op("vector", lambda e: e.tensor_tensor(out=tmp[:, 0:n].rearrange("p (g d) -> p g d", d=64),
                                                    in0=src.rearrange("p (g d) -> p g d", d=64),
                                                    in1=gains[:, gain_idx, :].unsqueeze(1).to_broadcast([128, G, 64]),
                                                    op=ALU.mult), r=[PS[bank], tC], w=[ttmp])
            P.op("vector", lambda e: e.tensor_scalar(out=f8[:, 8:8 + G], in0=f8[:, 0:G], scalar1=st[:, 5:6], scalar2=EPS,
                                                    op0=ALU.mult, op1=ALU.add), r=[tf8, tst], w=[tf8])
            P.op("scalar", lambda e: e.activation(out=f8[:, 16:16 + G], in_=f8[:, 8:8 + G], func=AF.Ln), r=[tf8], w=[tf8])
            P.op("scalar", lambda e: e.activation(out=f8[:, 24:24 + G], in_=f8[:, 16:16 + G], func=AF.Exp, scale=-0.5,
                                                  bias=st[:, 6:7]), r=[tf8, tst], w=[tf8])
            P.op("vector", lambda e: e.tensor_tensor(out=outb[:, 0:n].rearrange("p (g d) -> p g d", d=64),
                                                    in0=tmp[:, 0:n].rearrange("p (g d) -> p g d", d=64),
                                                    in1=f8[:, 24:24 + G].unsqueeze(2).to_broadcast([128, G, 64]),
                                                    op=ALU.mult), r=[ttmp, tf8], w=[toutb])

        with ExitStack() as sa:
            KT = sb(sa, "KT", [128, NB, 4, 128], BF16)
            VS = sb(sa, "VS", [128, NB, 4, 129], BF16)
            cop("gpsimd", lambda e: e.memset(VS[:].rearrange("p k h c -> p (k h) c")[:, :, 128:129], 1.0))
            W12 = sb(sa, "W12", [128, KC, 1024], BF16)
            tW12 = T(None)
            xt = [sb(sa, f"xt{i}", [128, D], F32) for i in range(2)]
            txt = [T(None), T(None)]
            stt = [sb(sa, f"st{i}", [128, 8], F32) for i in range(2)]
            tst = [T(None), T(None)]
            xb = sb(sa, "xb", [128, D], BF16)
            txb = T(None)
            xT = sb(sa, "xT", [128, KC, 128], BF16)
            txT = T(None)
            sq = sb(sa, "sq", [128, 512], F32)
            tsq = T(None)
            tmp = sb(sa, "tmp", [128, 512], F32)
            ttmp = T(None)
            f8 = sb(sa, "f8", [128, 32], F32)
            tf8 = T(None)
            knb = sb(sa, "knb", [128, 512], BF16)
            tknb = T(None)

            load_w(W12, tW12, [(C_KB, 512), (C_VB, 512)])
            xTs = [xT, sb(sa, "xT1", [128, KC, 128], BF16)]
            txTs = [txT, T(None)]
            knbs = [knb, sb(sa, "knb1", [128, 512], BF16)]
            tknbs = [tknb, T(None)]

            def p1_s0(kb):
                b2 = kb % 2
                xproc(xkv[kb * 128:(kb + 1) * 128, :], xt[b2], txt[b2], f"dx{b2}", stt[b2], tst[b2], xb, txb,
                      sq[:].bitcast(BF16), tsq, xTs[b2], txTs[b2], bank=4 * b2)

            def p1_s1(kb):
                b2 = kb % 2
                bk = 4 * b2
                proj(xTs[b2], txTs[b2], W12, tW12, 0, 512, bank=bk + 1)
                proj(xTs[b2], txTs[b2], W12, tW12, 512, 512, bank=bk + 2)
                qknorm(bk + 1, 0, 8, 3, stt[b2], tst[b2], sq, tsq, tmp, ttmp, f8, tf8, knbs[b2], tknbs[b2])
                P.op("scalar", lambda e, kb=kb, b2=b2, bk=bk: e.activation(
                    out=VS[:, kb, :, 0:128], in_=PSf[bk + 2].rearrange("p (h c) -> p h c", c=128), func=AF.Copy,
                    scale=stt[b2][:, 3:4]), r=[PS[bk + 2], tst[b2]], w=[tVS[kb]])

            def p1_s2(kb):
                b2 = kb % 2
                bk = 4 * b2
                pv = psb(bk + 3)
                P.grp("tensor", [lambda e, h=h, pv=pv, b2=b2: e.transpose(out=pv[:, h * 128:(h + 1) * 128],
                                                                         in_=knbs[b2][:, h * 128:(h + 1) * 128],
                                                                         identity=ident[:])
                                 for h in range(4)], r=[tknbs[b2], tC], w=[PS[bk + 3]])
                P.op("vector", lambda e, kb=kb, pv=pv: e.tensor_copy(out=KT[:, kb, :, :].rearrange("p h k -> p (h k)"),
                                                                    in_=pv[:, 0:512]), r=[PS[bk + 3]], w=[tKT[kb]])

            NB1 = NB if nphase >= 1 else 0
            for j in range(NB1 + 2):
                if j < NB1:
                    p1_s0(j)
                if 0 <= j - 1 < NB1:
                    p1_s1(j - 1)
                if 0 <= j - 2 < NB1:
                    p1_s2(j - 2)

            load_w(W12, tW12, [(C_QB, 512), (C_ZB, 512)])
            QT = sb(sa, "QT", [128, 4, 256], BF16)
            tQT = T(None)
            PT = [sb(sa, f"PT{i}", [128, 512], BF16) for i in range(3)]
            tPT = [T(None) for _ in range(3)]
            szb = [sb(sa, f"szb{i}", [128, 512], F32) for i in range(2)]
            tszb = [T(None), T(None)]
            osb = sb(sa, "osb", [128, 4, 129], F32)
            tosb = T(None)
            od = sb(sa, "od", [128, 2, 128], F32)
            tod = T(None)
            sm = sb(sa, "sm", [128, 16], F32)
            tsm = T(None)
            yt = sb(sa, "yt", [128, 128], F32)
            tyt = T(None)
            AB = [[4, 5], [4, 5]]
            QTs = [QT, sb(sa, "QT1", [128, 4, 256], BF16)]
            tQTs = [tQT, T(None)]
            hcount = 0
            NCH2 = NCH if nphase >= 2 else 0

            def preamble(i):
                cp = i % 2
                for t in range(2):
                    s_ = 2 * i + t
                    b2 = s_ % 2
                    xT_, txT_, knb_, tknb_ = xTs[t], txTs[t], knbs[t], tknbs[t]
                    xproc(xq[s_ * 128:(s_ + 1) * 128, :], xt[b2], txt[b2], f"dx{b2}", stt[b2], tst[b2], xb, txb,
                          sq[:].bitcast(BF16), tsq, xT_, txT_, bank=6)
                    yield
                    proj(xT_, txT_, W12, tW12, 0, 512, bank=7)
                    yield
                    qknorm(7, 0, 8, 2, stt[b2], tst[b2], sq, tsq, tmp, ttmp, f8, tf8, knb_, tknb_)
                    P.op("vector", lambda e, b2=b2, t=t: e.tensor_copy(out=zst[:, 2 * t:2 * t + 2], in_=stt[b2][:, 3:5]),
                         r=[tst[b2]], w=[tzst])
                    yield
                    pv = psb(7)
                    P.grp("tensor", [lambda e, h=h, pv=pv, knb_=knb_: e.transpose(out=pv[:, h * 128:(h + 1) * 128],
                                                                                 in_=knb_[:, h * 128:(h + 1) * 128],
                                                                                 identity=ident[:])
                                     for h in range(4)], r=[tknb_, tC], w=[PS[7]])
                    P.op("vector", lambda e, t=t, pv=pv, cp=cp: e.tensor_copy(
                        out=QTs[cp][:, :, t * 128:(t + 1) * 128], in_=pv[:, 0:512].rearrange("p (h k) -> p h k", k=128)),
                        r=[PS[7]], w=[tQTs[cp]])
                    yield

            zst = sb(sa, "zst", [128, 4], F32)
            tzst = T(None)

            def preamble_z(i):
                for t in range(2):
                    xT_, txT_ = xTs[t], txTs[t]
                    proj(xT_, txT_, W12, tW12, 512, 512, bank=6)
                    P.op("scalar", lambda e, t=t: e.activation(out=sq[:], in_=PSf[6], func=AF.Exp,
                                                              scale=zst[:, 2 * t + 1:2 * t + 2]), r=[PS[6], tzst], w=[tsq])
                    P.op("scalar", lambda e: e.activation(out=sq[:], in_=sq[:], func=AF.Ln, bias=1.0), r=[tsq], w=[tsq])
                    P.op("scalar", lambda e: e.activation(out=sq[:], in_=sq[:], func=AF.Exp, scale=-1.0), r=[tsq], w=[tsq])
                    P.op("vector", lambda e, t=t: e.tensor_scalar(out=tmp[:], in0=PSf[6], scalar1=zst[:, 2 * t:2 * t + 1],
                                                                 scalar2=None, op0=ALU.mult),
                         r=[PS[6], tzst], w=[ttmp])
                    P.op("vector", lambda e, t=t: e.tensor_tensor(out=szb[t][:], in0=tmp[:], in1=sq[:], op=ALU.mult),
                         r=[ttmp, tsq], w=[tszb[t]])

            def advance(gen, n=1):
                if gen is None:
                    return None
                for _ in range(n):
                    try:
                        next(gen)
                    except StopIteration:
                        return None
                return gen

            def epilogue(h, i, szb, tszb):
                P.op("vector", lambda e: e.reciprocal(out=sm[:, 0:4], in_=osb[:, :, 128]), r=[tosb], w=[tsm])
                P.op("vector", lambda e: e.tensor_scalar(out=sm[:, 2:4], in0=sm[:, 2:4], scalar1=lams[:, 4:5],
                                                        scalar2=None, op0=ALU.mult), r=[tsm, tC], w=[tsm])
                yield
                for t in range(2):
                    P.op("vector", lambda e, t=t: e.tensor_scalar(out=yt[:], in0=osb[:, 2 + t, 0:128],
                                                                 scalar1=sm[:, 2 + t:3 + t], scalar2=None, op0=ALU.mult),
                         r=[tosb, tsm], w=[tyt])
                    P.op("vector", lambda e, t=t: e.scalar_tensor_tensor(
                        out=od[:, t, :], in0=osb[:, t, 0:128], scalar=sm[:, t:t + 1], in1=yt[:],
                        op0=ALU.mult, op1=ALU.add), r=[tosb, tsm, tyt], w=[tod])
                    yield
                    P.op("scalar", lambda e, t=t: e.activation(out=yt[:], in_=od[:, t, :], func=AF.Square,
                                                              accum_out=sm[:, 4 + t:5 + t]), r=[tod], w=[tyt, tsm])
                    yield
                P.op("vector", lambda e: e.tensor_scalar(out=sm[:, 6:8], in0=sm[:, 4:6], scalar1=1.0 / 128, scalar2=EPS,
                                                        op0=ALU.mult, op1=ALU.add), r=[tsm], w=[tsm])
                yield
                P.op("scalar", lambda e: e.activation(out=sm[:, 8:10], in_=sm[:, 6:8], func=AF.Ln), r=[tsm], w=[tsm])
                P.op("scalar", lambda e: e.activation(out=sm[:, 10:12], in_=sm[:, 8:10], func=AF.Exp, scale=-0.5),
                     r=[tsm], w=[tsm])
                yield
                for t in range(2):
                    s_ = 2 * i + t
                    P.op("vector", lambda e, t=t: e.scalar_tensor_tensor(
                        out=yt[:], in0=od[:, t, :], scalar=sm[:, 10 + t:11 + t], in1=subg[:],
                        op0=ALU.mult, op1=ALU.mult), r=[tod, tsm, tC], w=[tyt])
                    P.op("vector", lambda e, t=t, s_=s_, h=h, szb=szb: e.tensor_tensor(
                        out=YB[:, s_, h * 128:(h + 1) * 128], in0=yt[:], in1=szb[t][:, h * 128:(h + 1) * 128],
                        op=ALU.mult), r=[tyt, tszb[t]], w=[tYB[s_]])
                    yield

            epi = None
            pre = preamble(0) if NCH2 > 0 else None
            while pre is not None:
                pre = advance(pre)
            for i in range(NCH2):
                nkb = 9 + 8 * i
                cp = i % 2
                QT, tQT = QTs[cp], tQTs[cp]
                preamble_z(i)
                pre = preamble(i + 1) if i + 1 < NCH2 else None
                every = max(1, (4 * nkb) // 10)
                gstep = 0
                for h in range(4):
                    ab = AB[hcount % 2]
                    hcount += 1
                    first_in_bank = {ab[0]: True, ab[1]: True}

                    def emit_S(kb, ss):
                        P.grp("tensor", [lambda e, m=m, kb=kb, ss=ss, h=h, QT=QT: e.matmul(
                            PP[ss][:, m, 0:256], lhsT=KT[m * 64:(m + 1) * 64, kb, h, :],
                            rhs=QT[m * 64:(m + 1) * 64, h, :], start=True, stop=True, skip_group_check=True)
                            for m in range(2)], r=[tKT[kb], tQT], w=[PS[2 * ss], PS[2 * ss + 1]])

                    def emit_E(kb, ss, slot):
                        if kb == 0:
                            bias = biasB0[:, h, i:i + 1]
                        else:
                            j = (2 + 8 * i - kb) + 6
                            bias = biasB[:, h, j:j + 1]
                        P.op("scalar", lambda e, ss=ss, slot=slot, bias=bias: e.activation(
                            out=PT[slot][:].rearrange("p (m c) -> p m c", m=2), in_=PP[ss][:, :, 0:256], func=AF.Exp,
                            bias=bias, scale=0.125),
                            r=[PS[2 * ss], PS[2 * ss + 1], tC], w=[tPT[slot]])
                        u = kb - 8 * i
                        if u >= 1:
                            dd0 = 8 - u
                            P.op("vector", lambda e, slot=slot, dd0=dd0: e.tensor_tensor(
                                out=PT[slot][:].rearrange("p (m t q) -> p m t q", m=2, t=2),
                                in0=PT[slot][:].rearrange("p (m t q) -> p m t q", m=2, t=2),
                                in1=MASK[:, dd0:dd0 + 2, :].unsqueeze(1).to_broadcast([128, 2, 2, 128]), op=ALU.mult),
                                r=[tC], w=[tPT[slot]])

                    def emit_AV(kb, slot):
                        fns = []
                        for m in range(2):
                            for t in range(2):
                                bank = ab[m]
                                st_flag = first_in_bank[bank]
                                first_in_bank[bank] = False
                                fns.append(lambda e, m=m, t=t, bank=bank, st_flag=st_flag, kb=kb, slot=slot, h=h: e.matmul(
                                    PSf[bank][:, t * 129:(t + 1) * 129],
                                    lhsT=PT[slot][:, (m * 2 + t) * 128:(m * 2 + t + 1) * 128],
                                    rhs=VS[:, kb, h, :], start=st_flag, stop=(kb == nkb - 1), skip_group_check=True))
                        P.grp("tensor", fns, r=[tPT[slot], tVS[kb]], w=[PS[ab[0]], PS[ab[1]]])

                    emit_S(0, 0)
                    for kb in range(nkb):
                        if kb + 1 < nkb:
                            emit_S(kb + 1, (kb + 1) % 2)
                        emit_E(kb, kb % 2, kb % 3)
                        emit_AV(kb, kb % 3)
                        gstep += 1
                        if gstep % every == 0:
                            pre = advance(pre)
                        if kb >= 1:
                            epi = advance(epi)
                    while epi is not None:
                        epi = advance(epi)
                    for m in range(2):
                        P.op("scalar", lambda e, m=m, ab=ab: e.activation(
                            out=osb[:, 2 * m:2 * m + 2, :].rearrange("p a c -> p (a c)"), in_=PSf[ab[m]][:, 0:258],
                            func=AF.Copy), r=[PS[ab[m]]], w=[tosb])
                    epi = epilogue(h, i, szb, tszb)
                while epi is not None:
                    epi = advance(epi)
                while pre is not None:
                    pre = advance(pre)

            if debug:
                dsc = sb(sa, "dsc", [128, 516], F32)
                tds = T(None)
                for kb in range(NB):
                    P.op("vector", lambda e, kb=kb: e.tensor_copy(out=dsc[:, 0:512], in_=KT[:, kb, :, :].rearrange("p h k -> p (h k)")),
                         r=[tKT[kb]], w=[tds])
                    P.op("sync", lambda e, kb=kb: e.dma_start(out=dbg["d_kt"][:, kb * 512:(kb + 1) * 512], in_=dsc[:, 0:512]),
                         r=[tds], dsem="dd")
                    P.op("vector", lambda e, kb=kb: e.tensor_copy(out=dsc[:, 0:516], in_=VS[:, kb, :, :].rearrange("p h k -> p (h k)")),
                         r=[tVS[kb]], w=[tds])
                    P.op("sync", lambda e, kb=kb: e.dma_start(out=dbg["d_v"][:, kb * 516:(kb + 1) * 516], in_=dsc[:, 0:516]),
                         r=[tds], dsem="dd")
                for s in range(NOWN):
                    P.op("vector", lambda e, s=s: e.tensor_copy(out=dsc[:, 0:512], in_=YB[:, s, :]), r=[tYB[s]], w=[tds])
                    P.op("sync", lambda e, s=s: e.dma_start(out=dbg["d_yb"][:, s * 512:(s + 1) * 512], in_=dsc[:, 0:512]),
                         r=[tds], dsem="dd")
                P.op("sync", lambda e: e.nop(), r=[tds], sig=False)
            P.run_block()

        with ExitStack() as sc:
            NW3 = 512 + 256 + 512 + 2048
            W3 = sb(sc, "W3", [128, KC, NW3], BF16)
            WUA = sb(sc, "WUA", [128, 4, D], BF16)
            WUB = sb(sc, "WUB", [128, 4, D], BF16)
            WO = sb(sc, "WO", [128, KC, D], BF16)
            tW3 = T(None)
            O_QA, O_KV, O_ZA, O_GL = 0, 512, 768, 1280
            tW3b, tWU, tWO = T(None), T(None), T(None)
            def load_w3a():
                load_w(W3, tW3, [(C_QA, 512), (C_KA, 256)], dsem="dw")

            def load_w3b():
                load_w(W3, tW3b, [(C_ZA, 512), (C_GL, 2048)], dsem="dw2", o0=768)

            def load_wuo():
                P.op("gpsimd", lambda e: e.dma_start(out=WUA[:], in_=w_up_a.rearrange("(kc p) c -> p kc c", p=128)), w=[tWU], dsem="dw3")
                P.op("gpsimd", lambda e: e.dma_start(out=WUB[:], in_=w_up_b.rearrange("(kc p) c -> p kc c", p=128)), w=[tWU], dsem="dw3")
                P.op("gpsimd", lambda e: e.dma_start(out=WO[:], in_=w_o.rearrange("(kc p) c -> p kc c", p=128)), w=[tWO], dsem="dw4")

            xt = [sb(sc, f"xt3_{i}", [128, D], F32) for i in range(2)]
            txt = [T(None), T(None)]
            xpt = [sb(sc, f"xpt{i}", [128, D], F32) for i in range(2)]
            txpt = [T(None), T(None)]
            stt = [sb(sc, f"st3_{i}", [128, 8], F32) for i in range(2)]
            tst = [T(None), T(None)]
            stp = [sb(sc, f"stp{i}", [128, 8], F32) for i in range(2)]
            tstp = [T(None), T(None)]
            xb = sb(sc, "xb3", [128, D], BF16)
            txb = T(None)
            xT = sb(sc, "xT3", [128, KC, 128], BF16)
            txT = T(None)
            xTp = sb(sc, "xTp", [128, KC, 128], BF16)
            txTp = T(None)
            sq = sb(sc, "sq3", [128, 512], F32)
            tsq = T(None)
            junk = sq[:].bitcast(BF16)
            tmp = sb(sc, "tmp3", [128, 512], F32)
            ttmp = T(None)
            f8 = sb(sc, "f83", [128, 32], F32)
            tf8 = T(None)
            qab = sb(sc, "qab", [128, 512], BF16)
            tqab = T(None)
            QaT = sb(sc, "QaT", [128, 4, 128], BF16)
            tQaT = T(None)
            kdup = sb(sc, "kdup", [128, 2, 2, 2, 64], BF16)
            tkdup = T(None)
            kab = sb(sc, "kab", [128, 128], BF16)
            tkab = T(None)
            KaT = sb(sc, "KaT", [128, 2, 2, 128], BF16)
            tKaT = T(None)
            Va = sb(sc, "Va", [128, 2, 2, 65], BF16)
            tVa = T(None)
            PTa = sb(sc, "PTa", [128, 2, 8, 128], BF16)
            tPTa = T(None)
            oa = sb(sc, "oa", [128, 8, 65], F32)
            toa = T(None)
            sm = sb(sc, "sm3", [128, 16], F32)
            tsm = T(None)
            ya = sb(sc, "ya", [128, 512], F32)
            tya = T(None)
            e1 = sb(sc, "e1", [128, 512], F32)
            te1 = T(None)
            e2 = sb(sc, "e2", [128, 512], F32)
            te2 = T(None)
            yab = sb(sc, "yab", [128, 512], BF16)
            tyab = T(None)
            yT = sb(sc, "yT", [128, KC, 128], BF16)
            tyT = T(None)
            mixb = sb(sc, "mixb", [128, D], BF16)
            tmixb = T(None)
            mixT = sb(sc, "mixT", [128, KC, 128], BF16)
            tmixT = T(None)
            ot = [sb(sc, f"ot{i}", [128, D], F32) for i in range(2)]
            tot = [T(None), T(None)]
            P.op("gpsimd", lambda e: e.memset(Va[:, :, :, 64:65].rearrange("p a b c -> p (a b) c"), 1.0), w=[tVa])

            xTs3 = [xT, sb(sc, "xT3b", [128, KC, 128], BF16), sb(sc, "xT3c", [128, KC, 128], BF16)]
            txTs3 = [txT, T(None), T(None)]
            xt = xt + [sb(sc, "xt3_2", [128, D], F32), sb(sc, "xt3_3", [128, D], F32)]
            txt = txt + [T(None), T(None)]
            xbo = [xb, sb(sc, "xbo1", [128, D], BF16)]
            txbo = [txb, T(None)]
            xbp = [sb(sc, "xbp0", [128, D], BF16), sb(sc, "xbp1", [128, D], BF16)]
            txbp = [T(None), T(None)]
            stt = stt + [sb(sc, "st3_2", [128, 8], F32), sb(sc, "st3_3", [128, 8], F32)]
            tst = tst + [T(None), T(None)]
            stp = stp + [sb(sc, "stp2", [128, 8], F32)]
            tstp = tstp + [T(None)]
            xTps = [xTp, sb(sc, "xTpb", [128, KC, 128], BF16)]
            txTps = [txTp, T(None)]
            yas = [ya, sb(sc, "ya_b", [128, 512], F32)]
            tyas = [tya, T(None)]
            junk3t = sb(sc, "junk3", [128, D], BF16)
            junk3 = junk3t[:]
            tjunk3 = T(None)

            def p3_load(s):
                b2 = s % 2
                b3 = s % 3
                b4 = s % 4
                xproc(xq[s * 128:(s + 1) * 128, :], xt[b4], txt[b4], f"dx{b4}", stt[b4], tst[b4], xbo[b2], txbo[b2],
                      junk3, tjunk3, None, None, bank=0)
                xproc(xp[s * 128:(s + 1) * 128, :], xpt[b2], txpt[b2], f"dp{b2}", stp[b3], tstp[b3], xbp[b2], txbp[b2],
                      junk3, tjunk3, None, None, bank=0)

            def p3_s0a(s):
                xproc_pe(xbo[s % 2], txbo[s % 2], xTs3[s % 3], txTs3[s % 3], 0)

            def p3_s0b(s):
                xproc_pe(xbp[s % 2], txbp[s % 2], xTps[s % 2], txTps[s % 2], 0)

            def sigm(bank, b3, dst, tdst):
                P.op("scalar", lambda e: e.activation(out=dst[:], in_=PSf[bank], func=AF.Exp, scale=stt[b3][:, 4:5]),
                     r=[PS[bank], tst[b3]], w=[tdst])
                P.op("scalar", lambda e: e.activation(out=dst[:], in_=dst[:], func=AF.Ln, bias=1.0), r=[tdst], w=[tdst])
                P.op("scalar", lambda e: e.activation(out=dst[:], in_=dst[:], func=AF.Exp, scale=-1.0), r=[tdst], w=[tdst])

            def s1_chunks(s):
                b2 = s % 2
                b3 = s % 3
                xT_, txT_, xTp_, txTp_ = xTs3[b3], txTs3[b3], xTps[b2], txTps[b2]

                def B1():
                    proj(xT_, txT_, W3, tW3, O_QA, 512, bank=1)
                    proj(xT_, txT_, W3, tW3, O_KV, 256, bank=2, boff=256, first=True)
                    proj(xTp_, txTp_, W3, tW3, O_KV, 256, bank=2, boff=0, first=False)

                def B2q():
                    qknorm(1, 0, 8, 0, stt[s % 4], tst[s % 4], sq, tsq, tmp, ttmp, f8, tf8, qab, tqab)

                def B3():
                    pv = psb(0)
                    P.grp("tensor", [lambda e, j=j, pv=pv: e.transpose(out=pv[:, j * 128:(j + 1) * 128],
                                                                      in_=qab[:, j * 128:(j + 1) * 128], identity=ident[:])
                                     for j in range(4)], r=[tqab, tC], w=[PS[0]])
                    P.op("vector", lambda e, pv=pv: e.tensor_copy(out=QaT[:].rearrange("p j k -> p (j k)"), in_=pv[:, 0:512]),
                         r=[PS[0]], w=[tQaT])

                def B2k():
                    for kblk in range(2):
                        stx, tstx = (stp[b3], tstp[b3]) if kblk == 0 else (stt[s % 4], tst[s % 4])
                        boff = 0 if kblk == 0 else 256
                        qknorm(2, boff, 2, 1, stx, tstx, sq, tsq, tmp, ttmp, f8, tf8, kab, tkab)
                        for dup in range(2):
                            P.op("gpsimd", lambda e, kblk=kblk, dup=dup: e.tensor_copy(
                                out=kdup[:, kblk, :, dup, :], in_=kab[:, 0:128].rearrange("p (h d) -> p h d", d=64)),
                                r=[tkab], w=[tkdup])
                        P.op("scalar", lambda e, kblk=kblk, boff=boff, stx=stx: e.activation(
                            out=Va[:, kblk, :, 0:64],
                            in_=PSf[2][:, boff + 128:boff + 256].rearrange("p (h d) -> p h d", d=64),
                            func=AF.Copy, scale=stx[:, 3:4]), r=[PS[2], tstx], w=[tVa])

                def B4():
                    pv = psb(0)
                    P.grp("tensor", [lambda e, kblk=kblk, kvh=kvh, pv=pv: e.transpose(
                        out=pv[:, (kblk * 2 + kvh) * 128:(kblk * 2 + kvh + 1) * 128],
                        in_=kdup[:, kblk, kvh, :, :].rearrange("p a d -> p (a d)"), identity=ident[:])
                        for kblk in range(2) for kvh in range(2)], r=[tkdup, tC], w=[PS[0]])
                    P.op("vector", lambda e, pv=pv: e.tensor_copy(out=KaT[:].rearrange("p a b k -> p (a b k)"),
                                                                 in_=pv[:, 0:512]), r=[PS[0]], w=[tKaT])

                def B5():
                    for kblk in range(2):
                        sbk = [4, 5] if kblk == 0 else [1, 2]
                        fns = []
                        for j in range(4):
                            kvh = j // 2
                            for ee in range(2):
                                fns.append(lambda e, kblk=kblk, kvh=kvh, j=j, ee=ee, sbk=sbk: e.matmul(
                                    PSf[sbk[ee]][:, j * 128:(j + 1) * 128],
                                    lhsT=KaT[ee * 64:(ee + 1) * 64, kblk, kvh, :],
                                    rhs=QaT[ee * 64:(ee + 1) * 64, j, :], start=True, stop=True, skip_group_check=True))
                        P.grp("tensor", fns, r=[tKaT, tQaT], w=[PS[sbk[0]], PS[sbk[1]]])
                        for j in range(4):
                            for ee in range(2):
                                hq = 2 * j + ee
                                P.op("scalar", lambda e, kblk=kblk, hq=hq, j=j, ee=ee, sbk=sbk: e.activation(
                                    out=PTa[:, kblk, hq, :], in_=PSf[sbk[ee]][:, j * 128:(j + 1) * 128], func=AF.Exp,
                                    bias=biasA[:, hq, kblk:kblk + 1], scale=0.125),
                                    r=[PS[sbk[ee]], tC], w=[tPTa])
                    P.op("vector", lambda e, s=s: e.scalar_tensor_tensor(
                        out=PTa[:, 0, :, :], in0=PTa[:, 0, :, :], scalar=kvalA[:, s:s + 1],
                        in1=Mprev[:].unsqueeze(1).to_broadcast([128, 8, 128]), op0=ALU.mult, op1=ALU.mult),
                        r=[tC], w=[tPTa])
                    P.op("vector", lambda e: e.tensor_tensor(
                        out=PTa[:, 1, :, :], in0=PTa[:, 1, :, :], in1=Mown[:].unsqueeze(1).to_broadcast([128, 8, 128]),
                        op=ALU.mult), r=[tC], w=[tPTa])

                def B6():
                    fib = {1: True, 2: True}
                    fns = []
                    for hq in range(8):
                        bank = 1 + hq // 4
                        hl = hq % 4
                        kvh = hq // 4
                        for kblk in range(2):
                            sf = fib[bank]
                            fib[bank] = False
                            fns.append(lambda e, hq=hq, hl=hl, kvh=kvh, kblk=kblk, bank=bank, sf=sf: e.matmul(
                                PSf[bank][:, hl * 65:(hl + 1) * 65], lhsT=PTa[:, kblk, hq, :], rhs=Va[:, kblk, kvh, :],
                                start=sf, stop=(kblk == 1), skip_group_check=True))
                    P.grp("tensor", fns, r=[tPTa, tVa], w=[PS[1], PS[2]])
                    for bnk in range(2):
                        P.op("scalar", lambda e, bnk=bnk: e.activation(
                            out=oa[:, bnk * 4:(bnk + 1) * 4, :].rearrange("p a c -> p (a c)"), in_=PSf[1 + bnk][:, 0:260],
                            func=AF.Copy), r=[PS[1 + bnk]], w=[toa])
                    P.op("vector", lambda e: e.tensor_tensor(out=sm[:, 0:8], in0=oa[:, :, 64], in1=sinkexp[:], op=ALU.add),
                         r=[toa, tC], w=[tsm])
                    P.op("vector", lambda e: e.reciprocal(out=sm[:, 8:16], in_=sm[:, 0:8]), r=[tsm], w=[tsm])
                    P.op("vector", lambda e: e.tensor_tensor(
                        out=yas[b2][:].rearrange("p (h d) -> p h d", d=64), in0=oa[:, :, 0:64],
                        in1=sm[:, 8:16].unsqueeze(2).to_broadcast([128, 8, 64]), op=ALU.mult),
                        r=[toa, tsm], w=[tyas[b2]])

                return dict(B1=B1, B2q=B2q, B2k=B2k, B3=B3, B4=B4, B5=B5, B6=B6)

            def s2_chunks(s):
                b2 = s % 2
                b3 = s % 3
                xT_, txT_ = xTs3[b3], txTs3[b3]

                def C1():
                    proj(xT_, txT_, W3, tW3b, O_ZA, 512, bank=6)
                    sigm(6, s % 4, e1, te1)
                    P.op("vector", lambda e: e.scalar_tensor_tensor(
                        out=e2[:], in0=PSf[6], scalar=stt[s % 4][:, 3:4], in1=yas[b2][:], op0=ALU.mult, op1=ALU.mult),
                        r=[PS[6], tst[s % 4], tyas[b2]], w=[te2])
                    P.op("vector", lambda e: e.tensor_tensor(out=yab[:], in0=e2[:], in1=e1[:], op=ALU.mult),
                         r=[te2, te1], w=[tyab])

                def C2():
                    pv = psb(6)
                    fns = []
                    for k in range(8):
                        src = yab[:, k * 128:(k + 1) * 128] if k < 4 else YB[:, s, (k - 4) * 128:(k - 3) * 128]
                        fns.append(lambda e, k=k, src=src, pv=pv: e.transpose(out=pv[:, k * 128:(k + 1) * 128], in_=src,
                                                                             identity=ident[:]))
                    P.grp("tensor", fns, r=[tyab, tYB[s], tC], w=[PS[6]])
                    P.op("vector", lambda e, pv=pv: e.tensor_copy(out=yT[:].rearrange("p k t -> p (k t)"),
                                                                 in_=pv[:, 0:1024]), r=[PS[6]], w=[tyT])

                def C3(g):
                    proj(xT_, txT_, W3, tW3b, O_GL + g * 512, 512, bank=7)
                    proj(xT_, txT_, W3, tW3b, O_GL + 1024 + g * 512, 512, bank=3)
                    sigm(7, s % 4, sgA[g], tsgA[g])
                    sigm(3, s % 4, sgB[g], tsgB[g])

                def C4a(g):
                    P.grp("tensor", [lambda e, k=k, g=g: e.matmul(PSf[6], lhsT=yT[:, k, :],
                                                                 rhs=WUA[:, k, g * 512:(g + 1) * 512],
                                                                 start=(k == 0), stop=(k == 3), skip_group_check=True)
                                     for k in range(4)], r=[tyT, tWU], w=[PS[6]])
                    P.op("vector", lambda e, g=g: e.tensor_tensor(out=sgA[g][:], in0=PSf[6], in1=sgA[g][:], op=ALU.mult),
                         r=[PS[6], tsgA[g]], w=[tsgA[g]])

                def C4b(g):
                    P.grp("tensor", [lambda e, k=k, g=g: e.matmul(PSf[0], lhsT=yT[:, 4 + k, :],
                                                                 rhs=WUB[:, k, g * 512:(g + 1) * 512],
                                                                 start=(k == 0), stop=(k == 3), skip_group_check=True)
                                     for k in range(4)], r=[tyT, tWU], w=[PS[0]])
                    P.op("vector", lambda e, g=g: e.tensor_tensor(out=sgB[g][:], in0=PSf[0], in1=sgB[g][:], op=ALU.mult),
                         r=[PS[0], tsgB[g]], w=[tsgB[g]])
                    P.op("vector", lambda e, g=g: e.tensor_tensor(out=mixb[:, g * 512:(g + 1) * 512], in0=sgA[g][:],
                                                                 in1=sgB[g][:], op=ALU.add),
                         r=[tsgA[g], tsgB[g]], w=[tmixb])

                def C5():
                    pv = psb(6)
                    P.grp("tensor", [lambda e, k=k, pv=pv: e.transpose(out=pv[:, k * 128:(k + 1) * 128],
                                                                      in_=mixb[:, k * 128:(k + 1) * 128], identity=ident[:])
                                     for k in range(8)], r=[tmixb, tC], w=[PS[6]])
                    P.op("vector", lambda e, pv=pv: e.tensor_copy(out=mixT[:].rearrange("p k t -> p (k t)"),
                                                                 in_=pv[:, 0:1024]), r=[PS[6]], w=[tmixT])
                    for g in range(2):
                        bank = 3 if g == 0 else 7
                        P.grp("tensor", [lambda e, k=k, g=g, bank=bank: e.matmul(
                            PSf[bank], lhsT=mixT[:, k, :], rhs=WO[:, k, g * 512:(g + 1) * 512],
                            start=(k == 0), stop=(k == 7), skip_group_check=True) for k in range(8)],
                            r=[tmixT, tWO], w=[PS[bank]])
                        P.op("vector", lambda e, g=g, bank=bank: e.tensor_tensor(
                            out=ot[b2][:, g * 512:(g + 1) * 512], in0=PSf[bank], in1=xt[s % 4][:, g * 512:(g + 1) * 512],
                            op=ALU.add), r=[PS[bank], txt[s % 4]], w=[tot[b2]])
                    P.op("sync", lambda e: e.dma_start(out=out[s * 128:(s + 1) * 128, :], in_=ot[b2][:]),
                         r=[tot[b2]], dsem=f"do{b2}")

                return dict(C1=C1, C2=C2, C3=C3, C4a=C4a, C4b=C4b, C5=C5)

            sgA = [e1, sb(sc, "sgA1", [128, 512], F32)]
            tsgA = [te1, T(None)]
            sgB = [sb(sc, "sgB0", [128, 512], F32), sb(sc, "sgB1", [128, 512], F32)]
            tsgB = [T(None), T(None)]

            N3 = NOWN if nphase >= 3 else 0
            for j in range(N3 + 2):
                A = j if j < N3 else None
                B = s1_chunks(j - 1) if 0 <= j - 1 < N3 else None
                C = s2_chunks(j - 2) if 0 <= j - 2 < N3 else None

                def run(d, name, *args):
                    if d is not None:
                        d[name](*args)

                if j == 0:
                    if N3 > 0:
                        p3_load(0)
                    load_w3a()
                    if N3 > 0:
                        p3_s0a(0)
                        p3_s0b(0)
                run(B, "B1")
                run(B, "B2q")
                run(B, "B2k")
                run(C, "C1")
                if j + 1 < N3:
                    p3_load(j + 1)
                if j == 0:
                    load_w3b()
                if j == 1 or (j == 0 and N3 < 2):
                    load_wuo()
                run(B, "B3")
                run(C, "C3", 0)
                run(C, "C2")
                run(B, "B4")
                run(C, "C4a", 0)
                run(B, "B5")
                run(C, "C4b", 0)
                run(C, "C3", 1)
                run(B, "B6")
                run(C, "C4a", 1)
                run(C, "C4b", 1)
                if j + 1 < N3:
                    p3_s0a(j + 1)
                    p3_s0b(j + 1)
                run(C, "C5")
            P.op("sync", lambda e: e.nop(), w=[tot[0], tot[1]], sig=False)
            P.run_block()
    return nc


def make_in_maps(inputs, NCH=8):
    NB = 1 + 8 * NCH
    x = np.asarray(inputs["x"], dtype=np.float32)
    meta = np.asarray(inputs["meta"], dtype=np.float32)
    B = x.shape[0]
    shared = {}
    for k in ["w_in", "w_up_a", "w_up_b", "w_o"]:
        shared[k] = np.ascontiguousarray(np.asarray(inputs[k], dtype=np.float32)[0])
    for k in ["norm_g", "a_qn", "a_kn", "a_sink", "b_qn", "b_kn", "b_lq1", "b_lk1", "b_lq2", "b_lk2", "b_subln"]:
        shared[k] = np.ascontiguousarray(np.asarray(inputs[k], dtype=np.float32).reshape(1, -1))
    in_maps = []
    blocks = []
    for core in range(8):
        b, c = core // 4, core % 4
        h = np.concatenate([np.zeros((NPAD, D), np.float32), meta, x[b]], axis=0)
        assert h.shape[0] == NB * 128
        own = [1 + 2 * (c + 4 * i) + t for i in range(NCH) for t in range(2)]
        xq = np.concatenate([h[n * 128:(n + 1) * 128] for n in own], axis=0)
        xp = np.concatenate([h[(n - 1) * 128:n * 128] for n in own], axis=0)
        cvec = np.zeros((128, 2), np.float32)
        cvec[:, 0] = 2 * c
        cvec[:, 1] = 128 * 2 * c
        m = dict(shared)
        m.update({"xkv": np.ascontiguousarray(h), "xq": np.ascontiguousarray(xq), "xp": np.ascontiguousarray(xp),
                  "cvec": cvec})
        in_maps.append(m)
        blocks.append((b, own))
    return in_maps, blocks


_CACHE = {}


def kernel(**inputs):
    NCH = 8
    x = np.asarray(inputs["x"])
    B, S, _ = x.shape
    in_maps, blocks = make_in_maps(inputs, NCH)
    if "nc" not in _CACHE:
        _CACHE["nc"] = build_program(NCH)
    nc = _CACHE["nc"]
    res = run_bass_kernel_spmd(nc, in_maps, core_ids=list(range(8)))
    outp = np.zeros((B, S, D), np.float32)
    for core in range(8):
        b, own = blocks[core]
        o = res.results[core]["out"]
        for si, n in enumerate(own):
            outp[b, (n - 1) * 128:n * 128, :] = o[si * 128:(si + 1) * 128, :]
    return outp
```

```python
from contextlib import ExitStack

import numpy as np
import concourse.bass as bass
import concourse.mybir as mybir
from concourse.bass_utils import run_bass_kernel_spmd

F32 = mybir.dt.float32
BF16 = mybir.dt.bfloat16
AF = mybir.ActivationFunctionType
ALU = mybir.AluOpType
AX = mybir.AxisListType

D = 1024
KC = 8
EPS = 1e-6
N_META = 16
NPAD = 128 - N_META
LAM_INIT = 0.8 - 0.6 * 1.0
NEGBIG = -30000.0

C_QA, C_KA, C_VA, C_ZA, C_QB, C_KB, C_VB, C_ZB, C_GL = 0, 512, 640, 768, 1280, 1792, 2304, 2816, 3328

ENGS = ["sync", "scalar", "vector", "gpsimd", "tensor"]


class T:
    def __init__(self, ap, excl=False):
        self.ap = ap
        self.w = None
        self.r = {}
        self.excl = excl


class Prog:
    def __init__(self, nc, sems):
        self.nc = nc
        self.sems = sems
        self.cnt = {k: 0 for k in sems}
        self.waited = {}
        self.q = {e: [] for e in ENGS}

    def op(self, eng, fn, r=(), w=(), dsem=None, extra=(), sig=True):
        toks = []
        for t in r:
            if t.w is not None:
                toks.append(t.w)
            if t.excl:
                for n, v in t.r.items():
                    if n != eng:
                        toks.append((n, v))
        for t in w:
            if t.w is not None:
                toks.append(t.w)
            for n, v in t.r.items():
                toks.append((n, v))
        toks.extend([x for x in extra if x is not None])
        ws = {}
        for (n, v) in toks:
            if eng == "tensor" and n == "tensor":
                continue
            if self.waited.get((eng, n), -1) >= v:
                continue
            ws[n] = max(ws.get(n, -1), v)
        for n, v in ws.items():
            self.waited[(eng, n)] = v
        tok = None
        sg = None
        if sig:
            if dsem is not None:
                sg = (dsem, 16)
            else:
                sg = (eng, 1)
            self.cnt[sg[0]] += sg[1]
            tok = (sg[0], self.cnt[sg[0]])
            for t in w:
                t.w = tok
                t.r = {}
            for t in r:
                t.r[tok[0]] = max(t.r.get(tok[0], -1), tok[1])
        sems = self.sems
        wl = list(ws.items())

        def emit(e, fn=fn, wl=wl, sg=sg):
            for (n, v) in wl:
                e.wait_ge(sems[n], v)
            ins = fn(e)
            if sg is not None:
                ins.then_inc(sems[sg[0]], sg[1])
        self.q[eng].append(emit)
        return tok

    def grp(self, eng, fns, r=(), w=()):
        n = len(fns)
        tok = None
        for i, fn in enumerate(fns):
            if n == 1:
                tok = self.op(eng, fn, r=r, w=w)
            elif i == 0:
                self.op(eng, fn, r=r, w=w, sig=False)
            elif i == n - 1:
                tok = self.op(eng, fn, r=r, w=w)
            else:
                self.op(eng, fn, sig=False)
        return tok

    def run_block(self):
        q = self.q
        with self.nc.Block() as block:
            @block.sync
            def _(e):
                for f in q["sync"]:
                    f(e)

            @block.scalar
            def _(e):
                for f in q["scalar"]:
                    f(e)

            @block.vector
            def _(e):
                for f in q["vector"]:
                    f(e)

            @block.gpsimd
            def _(e):
                for f in q["gpsimd"]:
                    f(e)

            @block.tensor
            def _(e):
                for f in q["tensor"]:
                    f(e)
        self.q = {e: [] for e in ENGS}


def build_program(NCH=8, debug=False, nphase=3):
    NB = 1 + 8 * NCH
    NOWN = 2 * NCH
    nc = bass.Bass("TRN2", target_bir_lowering=False)

    def din(name, shape):
        return nc.dram_tensor(name, shape, F32, kind="ExternalInput").ap()

    xkv = din("xkv", [NB * 128, D])
    xq = din("xq", [NOWN * 128, D])
    xp = din("xp", [NOWN * 128, D])
    cvec = din("cvec", [128, 2])
    w_in = din("w_in", [D, 5376])
    w_up_a = din("w_up_a", [512, D])
    w_up_b = din("w_up_b", [512, D])
    w_o = din("w_o", [D, D])
    norm_g = din("norm_g", [1, D])
    a_qn = din("a_qn", [1, 64])
    a_kn = din("a_kn", [1, 64])
    a_sink = din("a_sink", [1, 8])
    b_qn = din("b_qn", [1, 64])
    b_kn = din("b_kn", [1, 64])
    b_lq1 = din("b_lq1", [1, 64])
    b_lk1 = din("b_lk1", [1, 64])
    b_lq2 = din("b_lq2", [1, 64])
    b_lk2 = din("b_lk2", [1, 64])
    b_subln = din("b_subln", [1, 128])
    out = nc.dram_tensor("out", [NOWN * 128, D], F32, kind="ExternalOutput").ap()
    dbg = {}
    if debug:
        dbg["d_kt"] = nc.dram_tensor("d_kt", [128, NB * 512], F32, kind="ExternalOutput").ap()
        dbg["d_v"] = nc.dram_tensor("d_v", [128, NB * 516], F32, kind="ExternalOutput").ap()
        dbg["d_yb"] = nc.dram_tensor("d_yb", [128, NOWN * 512], F32, kind="ExternalOutput").ap()

    sem_names = ["sync", "scalar", "vector", "gpsimd", "tensor",
                 "dx0", "dx1", "dx2", "dx3", "dp0", "dp1", "dw", "dw2", "dw3", "dw4", "dc", "dc0", "do0", "do1", "dd"]

    with ExitStack() as es:
        sems = {n: es.enter_context(nc.semaphore(n)) for n in sem_names}
        P = Prog(nc, sems)

        def sb(scope, name, shape, dt):
            return scope.enter_context(nc.sbuf_tensor(name, shape, dt))

        YB = sb(es, "YB", [128, max(NOWN, 6), 512], BF16)
        g_rep = sb(es, "g_rep", [128, D], F32)
        ident = sb(es, "ident", [128, 128], BF16)
        ident_f0 = sb(es, "ident_f0", [128, 128], F32)
        cv = sb(es, "cv", [128, 2], F32)
        kio = sb(es, "kio", [128, 1], F32)
        gains = sb(es, "gains", [128, 4, 64], F32)
        lams = sb(es, "lams", [128, 8], F32)
        subg = sb(es, "subg", [128, 128], F32)
        sinkb = sb(es, "sinkb", [128, 8], F32)
        sinkexp = sb(es, "sinkexp", [128, 8], F32)
        NJ = 8 * NCH + 1
        biasB = sb(es, "biasB", [128, 4, NJ], F32)
        biasB0 = sb(es, "biasB0", [128, 4, NCH], F32)
        biasA = sb(es, "biasA", [128, 8, 2], F32)
        MASK = sb(es, "MASK", [128, 9, 128], BF16)
        Mown = sb(es, "Mown", [128, 128], BF16)
        Mprev = sb(es, "Mprev", [128, 128], BF16)
        kvalA = sb(es, "kvalA", [128, NOWN], F32)
        scr = YB[:].rearrange("p a c -> p (a c)").bitcast(F32)
        lamt = scr[:, 1280:1536].rearrange("p (a d) -> p a d", d=64)
        pad0 = sb(es, "pad0", [128, 1], F32)

        PP = [es.enter_context(nc.psum_tensor(f"pp{i}", [128, 2, 512], F32)) for i in range(4)]
        PSf = [PP[i // 2][:, i % 2, :] for i in range(8)]
        PS = [T(None, excl=True) for _ in range(8)]

        def psb(i):
            return PSf[i].bitcast(BF16)

        tKT = [T(None) for _ in range(NB)]
        tVS = [T(None) for _ in range(NB)]
        tYB = [T(None) for _ in range(NOWN)]
        tC = T(None)

        tC0 = T(None)

        def cload0(dst, src):
            P.op("sync", lambda e: e.dma_start(out=dst, in_=src), w=[tC0], dsem="dc0")

        cload0(g_rep[:], norm_g.partition_broadcast(128))
        cload0(gains[:, 3, :], b_kn.partition_broadcast(128))
        cload0(gains[:, 2, :], b_qn.partition_broadcast(128))
        cload0(gains[:, 0, :], a_qn.partition_broadcast(128))
        cload0(gains[:, 1, :], a_kn.partition_broadcast(128))
        cload0(cv[:], cvec)
        ident_scr = ident_f0
        P.op("gpsimd", lambda e: e.iota(ident_scr[:], pattern=[[1, 128]], base=0, channel_multiplier=-1,
                                       allow_small_or_imprecise_dtypes=True), w=[tC0])
        P.op("vector", lambda e: e.tensor_scalar(out=ident[:], in0=ident_scr[:], scalar1=0.0, scalar2=None,
                                                op0=ALU.is_equal), r=[tC0], w=[tC0])

        DEF = []

        def setup_rest():
            def cload(dst, src):
                DEF.append(lambda: P.op("sync", lambda e: e.dma_start(out=dst, in_=src), w=[tC], dsem="dc"))

            cload(lamt[:, 0, :], b_lq1.partition_broadcast(128))
            cload(lamt[:, 1, :], b_lk1.partition_broadcast(128))
            cload(lamt[:, 2, :], b_lq2.partition_broadcast(128))
            cload(lamt[:, 3, :], b_lk2.partition_broadcast(128))
            cload(subg[:], b_subln.partition_broadcast(128))
            cload(sinkb[:], a_sink.partition_broadcast(128))

            def cop(eng, fn):
                DEF.append(lambda: P.op(eng, fn, r=[tC, tC0], w=[tC]))

            cop("gpsimd", lambda e: e.iota(kio[:], pattern=[[0, 1]], base=0, channel_multiplier=1,
                                          allow_small_or_imprecise_dtypes=True))
            cop("gpsimd", lambda e: e.iota(scr[:, 0:128], pattern=[[1, 128]], base=0, channel_multiplier=-1,
                                          allow_small_or_imprecise_dtypes=True))
            cop("vector", lambda e: e.tensor_scalar(out=Mown[:], in0=scr[:, 0:128], scalar1=0.0, scalar2=None,
                                                   op0=ALU.is_ge))
            cop("vector", lambda e: e.tensor_scalar(out=Mprev[:], in0=scr[:, 0:128], scalar1=0.0, scalar2=None,
                                                   op0=ALU.is_lt))
            cop("gpsimd", lambda e: e.iota(scr[:, 0:9 * 128].rearrange("p (a q) -> p a q", q=128),
                                          pattern=[[128, 9], [1, 128]], base=-7 * 128, channel_multiplier=-1,
                                          allow_small_or_imprecise_dtypes=True))
            cop("vector", lambda e: e.tensor_scalar(out=MASK[:].rearrange("p a q -> p (a q)"), in0=scr[:, 0:9 * 128],
                                                   scalar1=cv[:, 1:2], scalar2=0.0, op0=ALU.add, op1=ALU.is_ge))
            cop("vector", lambda e: e.tensor_scalar(out=pad0[:], in0=kio[:], scalar1=float(NPAD), scalar2=NEGBIG,
                                                   op0=ALU.is_lt, op1=ALU.mult))
            cop("gpsimd", lambda e: e.iota(scr[:, 0:NJ], pattern=[[1, NJ]], base=-6, channel_multiplier=0,
                                          allow_small_or_imprecise_dtypes=True))
            cop("vector", lambda e: e.tensor_scalar(out=scr[:, 0:NJ], in0=scr[:, 0:NJ], scalar1=cv[:, 0:1], scalar2=0.0,
                                                   op0=ALU.add, op1=ALU.max))
            cop("vector", lambda e: e.tensor_scalar(out=scr[:, 0:NJ], in0=scr[:, 0:NJ], scalar1=-128.0, scalar2=kio[:, 0:1],
                                                   op0=ALU.mult, op1=ALU.add))
            for h in range(4):
                sl = 2.0 ** (-2 * (h + 1))
                cop("vector", lambda e, h=h, sl=sl: e.tensor_scalar(out=biasB[:, h, :], in0=scr[:, 0:NJ], scalar1=sl,
                                                                   scalar2=None, op0=ALU.mult))
            cop("gpsimd", lambda e: e.iota(scr[:, 256:256 + NCH], pattern=[[8, NCH]], base=2, channel_multiplier=0,
                                          allow_small_or_imprecise_dtypes=True))
            cop("vector", lambda e: e.tensor_scalar(out=scr[:, 256:256 + NCH], in0=scr[:, 256:256 + NCH], scalar1=cv[:, 0:1],
                                                   scalar2=-128.0, op0=ALU.add, op1=ALU.mult))
            cop("vector", lambda e: e.tensor_scalar(out=scr[:, 256:256 + NCH], in0=scr[:, 256:256 + NCH], scalar1=kio[:, 0:1],
                                                   scalar2=None, op0=ALU.add))
            for h in range(4):
                sl = 2.0 ** (-2 * (h + 1))
                cop("vector", lambda e, h=h, sl=sl: e.tensor_scalar(out=biasB0[:, h, :], in0=scr[:, 256:256 + NCH], scalar1=sl,
                                                                   scalar2=pad0[:, 0:1], op0=ALU.mult, op1=ALU.add))
            for hq in range(8):
                sl = 2.0 ** (-(hq + 1))
                for kblk in range(2):
                    off = -64.0 - (128.0 if kblk == 0 else 0.0)
                    cop("vector", lambda e, hq=hq, kblk=kblk, sl=sl, off=off: e.tensor_scalar(
                        out=biasA[:, hq, kblk:kblk + 1], in0=kio[:], scalar1=off, scalar2=sl, op0=ALU.add, op1=ALU.mult))
            for hq in range(8):
                sl = 2.0 ** (-(hq + 1))
                cop("vector", lambda e, hq=hq, sl=sl: e.tensor_scalar(out=sinkexp[:, hq:hq + 1], in0=kio[:], scalar1=-64.0,
                                                                     scalar2=sl, op0=ALU.add, op1=ALU.mult))
            cop("vector", lambda e: e.tensor_tensor(out=sinkexp[:], in0=sinkexp[:], in1=sinkb[:], op=ALU.add))
            cop("scalar", lambda e: e.activation(out=sinkexp[:], in_=sinkexp[:], func=AF.Exp))
            cop("gpsimd", lambda e: e.iota(scr[:, 512:512 + NOWN].rearrange("p (i t) -> p i t", t=2),
                                          pattern=[[1024, NCH], [128, 2]], base=-NPAD, channel_multiplier=1,
                                          allow_small_or_imprecise_dtypes=True))
            cop("vector", lambda e: e.tensor_scalar(out=kvalA[:], in0=scr[:, 512:512 + NOWN], scalar1=cv[:, 1:2], scalar2=0.0,
                                                   op0=ALU.add, op1=ALU.is_ge))
            cop("vector", lambda e: e.tensor_tensor(out=lamt[:, 0, :], in0=lamt[:, 0, :], in1=lamt[:, 1, :], op=ALU.mult))
            cop("vector", lambda e: e.tensor_tensor(out=lamt[:, 2, :], in0=lamt[:, 2, :], in1=lamt[:, 3, :], op=ALU.mult))
            cop("vector", lambda e: e.reduce_sum(out=lams[:, 0:1], in_=lamt[:, 0, :], axis=AX.X))
            cop("vector", lambda e: e.reduce_sum(out=lams[:, 1:2], in_=lamt[:, 2, :], axis=AX.X))
            cop("scalar", lambda e: e.activation(out=lams[:, 2:4], in_=lams[:, 0:2], func=AF.Exp))
            cop("vector", lambda e: e.tensor_tensor(out=lams[:, 4:5], in0=lams[:, 3:4], in1=lams[:, 2:3], op=ALU.subtract))
            cop("vector", lambda e: e.tensor_scalar(out=lams[:, 4:5], in0=lams[:, 4:5], scalar1=-LAM_INIT, scalar2=None,
                                                   op0=ALU.add))
            cop("vector", lambda e: e.tensor_scalar(out=subg[:], in0=subg[:], scalar1=1.0 - LAM_INIT, scalar2=None,
                                                   op0=ALU.mult))


        def load_w(dst_tile, tw, col_slices, dsem="dw", o0=0):
            o = o0
            for (c0, n) in col_slices:
                src = w_in[:, c0:c0 + n].rearrange("(kc p) c -> p kc c", p=128)
                P.op("gpsimd", lambda e, o=o, n=n, src=src: e.dma_start(out=dst_tile[:, :, o:o + n], in_=src),
                     w=[tw], dsem=dsem)
                o += n

        def xproc(xsrc_ap, xt, txt, dsem, st, tst, xb, txb, junk, tjunk, xT, txT, bank):
            P.op("sync", lambda e: e.dma_start(out=xt[:], in_=xsrc_ap), w=[txt], dsem=dsem)
            P.op("scalar", lambda e: e.activation(out=junk, in_=xt[:], func=AF.Square, accum_out=st[:, 0:1]),
                 r=[txt], w=[tjunk, tst])
            P.op("vector", lambda e: e.tensor_scalar(out=st[:, 1:2], in0=st[:, 0:1], scalar1=1.0 / D, scalar2=EPS,
                                                    op0=ALU.mult, op1=ALU.add), r=[tst], w=[tst])
            P.op("scalar", lambda e: e.activation(out=st[:, 2:3], in_=st[:, 1:2], func=AF.Ln), r=[tst], w=[tst])
            P.op("scalar", lambda e: e.activation(out=st[:, 3:4], in_=st[:, 2:3], func=AF.Exp, scale=-0.5),
                 r=[tst], w=[tst])
            P.op("vector", lambda e: e.tensor_scalar(out=st[:, 4:5], in0=st[:, 3:4], scalar1=-1.0, scalar2=None,
                                                    op0=ALU.mult), r=[tst], w=[tst])
            P.op("vector", lambda e: e.tensor_scalar(out=st[:, 5:6], in0=st[:, 3:4], scalar1=st[:, 3:4], scalar2=1.0 / 64,
                                                    op0=ALU.mult, op1=ALU.mult), r=[tst], w=[tst])
            P.op("vector", lambda e: e.tensor_scalar(out=st[:, 6:7], in0=st[:, 2:3], scalar1=-0.5, scalar2=None,
                                                    op0=ALU.mult), r=[tst], w=[tst])
            P.op("gpsimd", lambda e: e.tensor_tensor(out=xb[:], in0=xt[:], in1=g_rep[:], op=ALU.mult),
                 r=[txt, tC0], w=[txb])
            if xT is not None:
                xproc_pe(xb, txb, xT, txT, bank)

        def xproc_pe(xb, txb, xT, txT, bank):
            pv = psb(bank)
            P.grp("tensor", [lambda e, kc=kc: e.transpose(out=pv[:, kc * 128:(kc + 1) * 128],
                                                         in_=xb[:, kc * 128:(kc + 1) * 128], identity=ident[:])
                             for kc in range(KC)], r=[txb, tC0], w=[PS[bank]])
            P.op("vector", lambda e: e.tensor_copy(out=xT[:].rearrange("p k t -> p (k t)"), in_=pv[:, 0:1024]),
                 r=[PS[bank]], w=[txT])

        def proj(xT, txT, W, tW, c0, n, bank, boff=0, first=True):
            P.grp("tensor", [lambda e, kc=kc: e.matmul(PSf[bank][:, boff:boff + n], lhsT=xT[:, kc, :],
                                                      rhs=W[:, kc, c0:c0 + n], start=(kc == 0 and first),
                                                      stop=(kc == KC - 1), skip_group_check=True)
                             for kc in range(KC)], r=[txT, tW], w=[PS[bank]])

        def qknorm(bank, boff, G, gain_idx, st, tst, sq, tsq, tmp, ttmp, f8, tf8, outb, toutb):
            n = G * 64
            src = PSf[bank][:, boff:boff + n]
            P.op("scalar", lambda e: e.activation(out=sq[:, 0:n], in_=src, func=AF.Square), r=[PS[bank]], w=[tsq])
            P.op("vector", lambda e: e.reduce_sum(out=f8[:, 0:G], in_=sq[:, 0:n].rearrange("p (g d) -> p g d", d=64),
                                                 axis=AX.X), r=[tsq], w=[tf8])
            P.op("vector", lambda e: e.tensor_tensor(out=tmp[:, 0:n].rearrange("p (g d) -> p g d", d=64),
                                                    in0=src.rearrange("p (g d) -> p g d", d=64),
                                                    in1=gains[:, gain_idx, :].unsqueeze(1).to_broadcast([128, G, 64]),
                                                    op=ALU.mult), r=[PS[bank], tC0], w=[ttmp])
            P.op("vector", lambda e: e.tensor_scalar(out=f8[:, 8:8 + G], in0=f8[:, 0:G], scalar1=st[:, 5:6], scalar2=EPS,
                                                    op0=ALU.mult, op1=ALU.add), r=[tf8, tst], w=[tf8])
            P.op("scalar", lambda e: e.activation(out=f8[:, 16:16 + G], in_=f8[:, 8:8 + G], func=AF.Ln), r=[tf8], w=[tf8])
            P.op("scalar", lambda e: e.activation(out=f8[:, 24:24 + G], in_=f8[:, 16:16 + G], func=AF.Exp, scale=-0.5,
                                                  bias=st[:, 6:7]), r=[tf8, tst], w=[tf8])
            P.op("vector", lambda e: e.tensor_tensor(out=outb[:, 0:n].rearrange("p (g d) -> p g d", d=64),
                                                    in0=tmp[:, 0:n].rearrange("p (g d) -> p g d", d=64),
                                                    in1=f8[:, 24:24 + G].unsqueeze(2).to_broadcast([128, G, 64]),
                                                    op=ALU.mult), r=[ttmp, tf8], w=[toutb])

        with ExitStack() as sa:
            KT = sb(sa, "KT", [128, NB, 4, 128], BF16)
            VS = sb(sa, "VS", [128, NB, 4, 129], BF16)
            W12 = sb(sa, "W12", [128, KC, 1024], BF16)
            tW12 = T(None)
            xt = [sb(sa, f"xt{i}", [128, D], F32) for i in range(2)]
            txt = [T(None), T(None)]
            stt = [sb(sa, f"st{i}", [128, 8], F32) for i in range(2)]
            tst = [T(None), T(None)]
            xb = sb(sa, "xb", [128, D], BF16)
            txb = T(None)
            xT = sb(sa, "xT", [128, KC, 128], BF16)
            txT = T(None)
            sq = sb(sa, "sq", [128, 512], F32)
            tsq = T(None)
            tmp = sb(sa, "tmp", [128, 512], F32)
            ttmp = T(None)
            f8 = sb(sa, "f8", [128, 32], F32)
            tf8 = T(None)
            knb = sb(sa, "knb", [128, 512], BF16)
            tknb = T(None)

            load_w(W12, tW12, [(C_KB, 512), (C_VB, 512)])
            xTs = [xT, sb(sa, "xT1", [128, KC, 128], BF16)]
            txTs = [txT, T(None)]
            knbs = [knb, sb(sa, "knb1", [128, 512], BF16)]
            tknbs = [tknb, T(None)]

            def p1_s0(kb):
                b2 = kb % 2
                xproc(xkv[kb * 128:(kb + 1) * 128, :], xt[b2], txt[b2], f"dx{b2}", stt[b2], tst[b2], xb, txb,
                      sq[:].bitcast(BF16), tsq, xTs[b2], txTs[b2], bank=4 * b2)

            def p1_s1(kb):
                b2 = kb % 2
                bk = 4 * b2
                proj(xTs[b2], txTs[b2], W12, tW12, 0, 512, bank=bk + 1)
                proj(xTs[b2], txTs[b2], W12, tW12, 512, 512, bank=bk + 2)
                qknorm(bk + 1, 0, 8, 3, stt[b2], tst[b2], sq, tsq, tmp, ttmp, f8, tf8, knbs[b2], tknbs[b2])
                P.op("scalar", lambda e, kb=kb, b2=b2, bk=bk: e.activation(
                    out=VS[:, kb, :, 0:128], in_=PSf[bk + 2].rearrange("p (h c) -> p h c", c=128), func=AF.Copy,
                    scale=stt[b2][:, 3:4]), r=[PS[bk + 2], tst[b2]], w=[tVS[kb]])

            def p1_s2(kb):
                b2 = kb % 2
                bk = 4 * b2
                pv = psb(bk + 3)
                P.grp("tensor", [lambda e, h=h, pv=pv, b2=b2: e.transpose(out=pv[:, h * 128:(h + 1) * 128],
                                                                         in_=knbs[b2][:, h * 128:(h + 1) * 128],
                                                                         identity=ident[:])
                                 for h in range(4)], r=[tknbs[b2], tC0], w=[PS[bk + 3]])
                P.op("vector", lambda e, kb=kb, pv=pv: e.tensor_copy(out=KT[:, kb, :, :].rearrange("p h k -> p (h k)"),
                                                                    in_=pv[:, 0:512]), r=[PS[bk + 3]], w=[tKT[kb]])

            NB1 = NB if nphase >= 1 else 0
            for j in range(NB1 + 2):
                if j == 0:
                    setup_rest()
                    DEF.append(lambda: P.op("gpsimd", lambda e: e.memset(
                        VS[:].rearrange("p k h c -> p (k h) c")[:, :, 128:129], 1.0), r=[tC], w=[tC]))
                if j >= 3:
                    for _ in range(3):
                        if DEF:
                            DEF.pop(0)()
                if j < NB1:
                    p1_s0(j)
                if 0 <= j - 1 < NB1:
                    p1_s1(j - 1)
                if 0 <= j - 2 < NB1:
                    p1_s2(j - 2)

            while DEF:
                DEF.pop(0)()

            load_w(W12, tW12, [(C_QB, 512), (C_ZB, 512)])
            QT = sb(sa, "QT", [128, 4, 256], BF16)
            tQT = T(None)
            PT = [sb(sa, f"PT{i}", [128, 512], BF16) for i in range(3)]
            tPT = [T(None) for _ in range(3)]
            szb = [sb(sa, f"szb{i}", [128, 512], F32) for i in range(2)]
            tszb = [T(None), T(None)]
            osb = sb(sa, "osb", [128, 4, 129], F32)
            tosb = T(None)
            od = sb(sa, "od", [128, 2, 128], F32)
            tod = T(None)
            sm = sb(sa, "sm", [128, 16], F32)
            tsm = T(None)
            yt = sb(sa, "yt", [128, 128], F32)
            tyt = T(None)
            AB = [[4, 5], [4, 5]]
            QTs = [QT, sb(sa, "QT1", [128, 4, 256], BF16)]
            tQTs = [tQT, T(None)]
            hcount = 0
            NCH2 = NCH if nphase >= 2 else 0

            def preamble(i):
                cp = i % 2
                for t in range(2):
                    s_ = 2 * i + t
                    b2 = s_ % 2
                    xT_, txT_, knb_, tknb_ = xTs[t], txTs[t], knbs[t], tknbs[t]
                    xproc(xq[s_ * 128:(s_ + 1) * 128, :], xt[b2], txt[b2], f"dx{b2}", stt[b2], tst[b2], xb, txb,
                          sq[:].bitcast(BF16), tsq, xT_, txT_, bank=6)
                    yield
                    proj(xT_, txT_, W12, tW12, 0, 512, bank=7)
                    yield
                    qknorm(7, 0, 8, 2, stt[b2], tst[b2], sq, tsq, tmp, ttmp, f8, tf8, knb_, tknb_)
                    P.op("vector", lambda e, b2=b2, t=t: e.tensor_copy(out=zst[:, 2 * t:2 * t + 2], in_=stt[b2][:, 3:5]),
                         r=[tst[b2]], w=[tzst])
                    yield
                    pv = psb(7)
                    P.grp("tensor", [lambda e, h=h, pv=pv, knb_=knb_: e.transpose(out=pv[:, h * 128:(h + 1) * 128],
                                                                                 in_=knb_[:, h * 128:(h + 1) * 128],
                                                                                 identity=ident[:])
                                     for h in range(4)], r=[tknb_, tC], w=[PS[7]])
                    P.op("vector", lambda e, t=t, pv=pv, cp=cp: e.tensor_copy(
                        out=QTs[cp][:, :, t * 128:(t + 1) * 128], in_=pv[:, 0:512].rearrange("p (h k) -> p h k", k=128)),
                        r=[PS[7]], w=[tQTs[cp]])
                    yield

            zst = sb(sa, "zst", [128, 4], F32)
            tzst = T(None)

            def preamble_z(i):
                for t in range(2):
                    xT_, txT_ = xTs[t], txTs[t]
                    proj(xT_, txT_, W12, tW12, 512, 512, bank=6)
                    P.op("scalar", lambda e, t=t: e.activation(out=sq[:], in_=PSf[6], func=AF.Exp,
                                                              scale=zst[:, 2 * t + 1:2 * t + 2]), r=[PS[6], tzst], w=[tsq])
                    P.op("scalar", lambda e: e.activation(out=sq[:], in_=sq[:], func=AF.Ln, bias=1.0), r=[tsq], w=[tsq])
                    P.op("scalar", lambda e: e.activation(out=sq[:], in_=sq[:], func=AF.Exp, scale=-1.0), r=[tsq], w=[tsq])
                    P.op("vector", lambda e, t=t: e.tensor_scalar(out=tmp[:], in0=PSf[6], scalar1=zst[:, 2 * t:2 * t + 1],
                                                                 scalar2=None, op0=ALU.mult),
                         r=[PS[6], tzst], w=[ttmp])
                    P.op("vector", lambda e, t=t: e.tensor_tensor(out=szb[t][:], in0=tmp[:], in1=sq[:], op=ALU.mult),
                         r=[ttmp, tsq], w=[tszb[t]])

            def advance(gen, n=1):
                if gen is None:
                    return None
                for _ in range(n):
                    try:
                        next(gen)
                    except StopIteration:
                        return None
                return gen

            def epilogue(h, i, szb, tszb):
                P.op("vector", lambda e: e.reciprocal(out=sm[:, 0:4], in_=osb[:, :, 128]), r=[tosb], w=[tsm])
                P.op("vector", lambda e: e.tensor_scalar(out=sm[:, 2:4], in0=sm[:, 2:4], scalar1=lams[:, 4:5],
                                                        scalar2=None, op0=ALU.mult), r=[tsm, tC], w=[tsm])
                yield
                for t in range(2):
                    P.op("vector", lambda e, t=t: e.tensor_scalar(out=yt[:], in0=osb[:, 2 + t, 0:128],
                                                                 scalar1=sm[:, 2 + t:3 + t], scalar2=None, op0=ALU.mult),
                         r=[tosb, tsm], w=[tyt])
                    P.op("vector", lambda e, t=t: e.scalar_tensor_tensor(
                        out=od[:, t, :], in0=osb[:, t, 0:128], scalar=sm[:, t:t + 1], in1=yt[:],
                        op0=ALU.mult, op1=ALU.add), r=[tosb, tsm, tyt], w=[tod])
                    yield
                    P.op("scalar", lambda e, t=t: e.activation(out=yt[:], in_=od[:, t, :], func=AF.Square,
                                                              accum_out=sm[:, 4 + t:5 + t]), r=[tod], w=[tyt, tsm])
                    yield
                P.op("vector", lambda e: e.tensor_scalar(out=sm[:, 6:8], in0=sm[:, 4:6], scalar1=1.0 / 128, scalar2=EPS,
                                                        op0=ALU.mult, op1=ALU.add), r=[tsm], w=[tsm])
                yield
                P.op("scalar", lambda e: e.activation(out=sm[:, 8:10], in_=sm[:, 6:8], func=AF.Ln), r=[tsm], w=[tsm])
                P.op("scalar", lambda e: e.activation(out=sm[:, 10:12], in_=sm[:, 8:10], func=AF.Exp, scale=-0.5),
                     r=[tsm], w=[tsm])
                yield
                for t in range(2):
                    s_ = 2 * i + t
                    P.op("vector", lambda e, t=t: e.scalar_tensor_tensor(
                        out=yt[:], in0=od[:, t, :], scalar=sm[:, 10 + t:11 + t], in1=subg[:],
                        op0=ALU.mult, op1=ALU.mult), r=[tod, tsm, tC], w=[tyt])
                    P.op("vector", lambda e, t=t, s_=s_, h=h, szb=szb: e.tensor_tensor(
                        out=YB[:, s_, h * 128:(h + 1) * 128], in0=yt[:], in1=szb[t][:, h * 128:(h + 1) * 128],
                        op=ALU.mult), r=[tyt, tszb[t]], w=[tYB[s_]])
                    yield

            epi = None
            pre = preamble(0) if NCH2 > 0 else None
            while pre is not None:
                pre = advance(pre)
            for i in range(NCH2):
                nkb = 9 + 8 * i
                cp = i % 2
                QT, tQT = QTs[cp], tQTs[cp]
                preamble_z(i)
                pre = preamble(i + 1) if i + 1 < NCH2 else None
                every = max(1, (4 * nkb) // 10)
                gstep = 0
                for h in range(4):
                    ab = AB[hcount % 2]
                    hcount += 1
                    first_in_bank = {ab[0]: True, ab[1]: True}

                    def emit_S(kb, ss):
                        P.grp("tensor", [lambda e, m=m, kb=kb, ss=ss, h=h, QT=QT: e.matmul(
                            PP[ss][:, m, 0:256], lhsT=KT[m * 64:(m + 1) * 64, kb, h, :],
                            rhs=QT[m * 64:(m + 1) * 64, h, :], start=True, stop=True, skip_group_check=True)
                            for m in range(2)], r=[tKT[kb], tQT], w=[PS[2 * ss], PS[2 * ss + 1]])

                    def emit_E(kb, ss, slot):
                        if kb == 0:
                            bias = biasB0[:, h, i:i + 1]
                        else:
                            j = (2 + 8 * i - kb) + 6
                            bias = biasB[:, h, j:j + 1]
                        P.op("scalar", lambda e, ss=ss, slot=slot, bias=bias: e.activation(
                            out=PT[slot][:].rearrange("p (m c) -> p m c", m=2), in_=PP[ss][:, :, 0:256], func=AF.Exp,
                            bias=bias, scale=0.125),
                            r=[PS[2 * ss], PS[2 * ss + 1], tC], w=[tPT[slot]])
                        u = kb - 8 * i
                        if u >= 1:
                            dd0 = 8 - u
                            P.op("vector", lambda e, slot=slot, dd0=dd0: e.tensor_tensor(
                                out=PT[slot][:].rearrange("p (m t q) -> p m t q", m=2, t=2),
                                in0=PT[slot][:].rearrange("p (m t q) -> p m t q", m=2, t=2),
                                in1=MASK[:, dd0:dd0 + 2, :].unsqueeze(1).to_broadcast([128, 2, 2, 128]), op=ALU.mult),
                                r=[tC], w=[tPT[slot]])

                    def emit_AV(kb, slot):
                        fns = []
                        for m in range(2):
                            for t in range(2):
                                bank = ab[m]
                                st_flag = first_in_bank[bank]
                                first_in_bank[bank] = False
                                fns.append(lambda e, m=m, t=t, bank=bank, st_flag=st_flag, kb=kb, slot=slot, h=h: e.matmul(
                                    PSf[bank][:, t * 129:(t + 1) * 129],
                                    lhsT=PT[slot][:, (m * 2 + t) * 128:(m * 2 + t + 1) * 128],
                                    rhs=VS[:, kb, h, :], start=st_flag, stop=(kb == nkb - 1), skip_group_check=True))
                        P.grp("tensor", fns, r=[tPT[slot], tVS[kb]], w=[PS[ab[0]], PS[ab[1]]])

                    emit_S(0, 0)
                    for kb in range(nkb):
                        if kb + 1 < nkb:
                            emit_S(kb + 1, (kb + 1) % 2)
                        emit_E(kb, kb % 2, kb % 3)
                        emit_AV(kb, kb % 3)
                        gstep += 1
                        if gstep % every == 0:
                            pre = advance(pre)
                        if kb >= 1:
                            epi = advance(epi)
                    while epi is not None:
                        epi = advance(epi)
                    for m in range(2):
                        P.op("scalar", lambda e, m=m, ab=ab: e.activation(
                            out=osb[:, 2 * m:2 * m + 2, :].rearrange("p a c -> p (a c)"), in_=PSf[ab[m]][:, 0:258],
                            func=AF.Copy), r=[PS[ab[m]]], w=[tosb])
                    epi = epilogue(h, i, szb, tszb)
                while epi is not None:
                    epi = advance(epi)
                while pre is not None:
                    pre = advance(pre)

            if debug:
                dsc = sb(sa, "dsc", [128, 516], F32)
                tds = T(None)
                for kb in range(NB):
                    P.op("vector", lambda e, kb=kb: e.tensor_copy(out=dsc[:, 0:512], in_=KT[:, kb, :, :].rearrange("p h k -> p (h k)")),
                         r=[tKT[kb]], w=[tds])
                    P.op("sync", lambda e, kb=kb: e.dma_start(out=dbg["d_kt"][:, kb * 512:(kb + 1) * 512], in_=dsc[:, 0:512]),
                         r=[tds], dsem="dd")
                    P.op("vector", lambda e, kb=kb: e.tensor_copy(out=dsc[:, 0:516], in_=VS[:, kb, :, :].rearrange("p h k -> p (h k)")),
                         r=[tVS[kb]], w=[tds])
                    P.op("sync", lambda e, kb=kb: e.dma_start(out=dbg["d_v"][:, kb * 516:(kb + 1) * 516], in_=dsc[:, 0:516]),
                         r=[tds], dsem="dd")
                for s in range(NOWN):
                    P.op("vector", lambda e, s=s: e.tensor_copy(out=dsc[:, 0:512], in_=YB[:, s, :]), r=[tYB[s]], w=[tds])
                    P.op("sync", lambda e, s=s: e.dma_start(out=dbg["d_yb"][:, s * 512:(s + 1) * 512], in_=dsc[:, 0:512]),
                         r=[tds], dsem="dd")
                P.op("sync", lambda e: e.nop(), r=[tds], sig=False)
            P.run_block()

        with ExitStack() as sc:
            NW3 = 512 + 256 + 512 + 2048
            W3 = sb(sc, "W3", [128, KC, NW3], BF16)
            WUA = sb(sc, "WUA", [128, 4, D], BF16)
            WUB = sb(sc, "WUB", [128, 4, D], BF16)
            WO = sb(sc, "WO", [128, KC, D], BF16)
            tW3 = T(None)
            O_QA, O_KV, O_ZA, O_GL = 0, 512, 768, 1280
            tW3b, tWU, tWO = T(None), T(None), T(None)
            def load_w3a():
                load_w(W3, tW3, [(C_QA, 512), (C_KA, 256)], dsem="dw")

            def load_w3b():
                load_w(W3, tW3b, [(C_ZA, 512), (C_GL, 2048)], dsem="dw2", o0=768)

            def load_wuo():
                P.op("gpsimd", lambda e: e.dma_start(out=WUA[:], in_=w_up_a.rearrange("(kc p) c -> p kc c", p=128)), w=[tWU], dsem="dw3")
                P.op("gpsimd", lambda e: e.dma_start(out=WUB[:], in_=w_up_b.rearrange("(kc p) c -> p kc c", p=128)), w=[tWU], dsem="dw3")
                P.op("gpsimd", lambda e: e.dma_start(out=WO[:], in_=w_o.rearrange("(kc p) c -> p kc c", p=128)), w=[tWO], dsem="dw4")

            xt = [sb(sc, f"xt3_{i}", [128, D], F32) for i in range(2)]
            txt = [T(None), T(None)]
            xpt = [sb(sc, f"xpt{i}", [128, D], F32) for i in range(2)]
            txpt = [T(None), T(None)]
            stt = [sb(sc, f"st3_{i}", [128, 8], F32) for i in range(2)]
            tst = [T(None), T(None)]
            stp = [sb(sc, f"stp{i}", [128, 8], F32) for i in range(2)]
            tstp = [T(None), T(None)]
            xb = sb(sc, "xb3", [128, D], BF16)
            txb = T(None)
            xT = sb(sc, "xT3", [128, KC, 128], BF16)
            txT = T(None)
            xTp = sb(sc, "xTp", [128, KC, 128], BF16)
            txTp = T(None)
            sq = sb(sc, "sq3", [128, 512], F32)
            tsq = T(None)
            junk = sq[:].bitcast(BF16)
            tmp = sb(sc, "tmp3", [128, 512], F32)
            ttmp = T(None)
            f8 = sb(sc, "f83", [128, 32], F32)
            tf8 = T(None)
            qab = sb(sc, "qab", [128, 512], BF16)
            tqab = T(None)
            QaT = sb(sc, "QaT", [128, 4, 128], BF16)
            tQaT = T(None)
            kdup = sb(sc, "kdup", [128, 2, 2, 2, 64], BF16)
            tkdup = T(None)
            kab = sb(sc, "kab", [128, 128], BF16)
            tkab = T(None)
            KaT = sb(sc, "KaT", [128, 2, 2, 128], BF16)
            tKaT = T(None)
            Va = sb(sc, "Va", [128, 2, 2, 65], BF16)
            tVa = T(None)
            PTa = sb(sc, "PTa", [128, 2, 8, 128], BF16)
            tPTa = T(None)
            oa = sb(sc, "oa", [128, 8, 65], F32)
            toa = T(None)
            sm = sb(sc, "sm3", [128, 16], F32)
            tsm = T(None)
            ya = sb(sc, "ya", [128, 512], F32)
            tya = T(None)
            e1 = sb(sc, "e1", [128, 512], F32)
            te1 = T(None)
            e2 = sb(sc, "e2", [128, 512], F32)
            te2 = T(None)
            yab = sb(sc, "yab", [128, 512], BF16)
            tyab = T(None)
            yT = sb(sc, "yT", [128, KC, 128], BF16)
            tyT = T(None)
            mixb = sb(sc, "mixb", [128, D], BF16)
            tmixb = T(None)
            mixT = sb(sc, "mixT", [128, KC, 128], BF16)
            tmixT = T(None)
            ot = [sb(sc, f"ot{i}", [128, D], F32) for i in range(2)]
            tot = [T(None), T(None)]
            P.op("gpsimd", lambda e: e.memset(Va[:, :, :, 64:65].rearrange("p a b c -> p (a b) c"), 1.0), w=[tVa])

            xTs3 = [xT, sb(sc, "xT3b", [128, KC, 128], BF16), sb(sc, "xT3c", [128, KC, 128], BF16)]
            txTs3 = [txT, T(None), T(None)]
            xt = xt + [sb(sc, "xt3_2", [128, D], F32), sb(sc, "xt3_3", [128, D], F32)]
            txt = txt + [T(None), T(None)]
            xbo = [xb, sb(sc, "xbo1", [128, D], BF16)]
            txbo = [txb, T(None)]
            xbp = [sb(sc, "xbp0", [128, D], BF16), sb(sc, "xbp1", [128, D], BF16)]
            txbp = [T(None), T(None)]
            stt = stt + [sb(sc, "st3_2", [128, 8], F32), sb(sc, "st3_3", [128, 8], F32)]
            tst = tst + [T(None), T(None)]
            stp = stp + [sb(sc, "stp2", [128, 8], F32)]
            tstp = tstp + [T(None)]
            xTps = [xTp, sb(sc, "xTpb", [128, KC, 128], BF16)]
            txTps = [txTp, T(None)]
            yas = [ya, sb(sc, "ya_b", [128, 512], F32)]
            tyas = [tya, T(None)]
            junk3t = sb(sc, "junk3", [128, D], BF16)
            junk3 = junk3t[:]
            tjunk3 = T(None)

            def p3_load(s):
                b2 = s % 2
                b3 = s % 3
                b4 = s % 4
                xproc(xq[s * 128:(s + 1) * 128, :], xt[b4], txt[b4], f"dx{b4}", stt[b4], tst[b4], xbo[b2], txbo[b2],
                      junk3, tjunk3, None, None, bank=0)
                xproc(xp[s * 128:(s + 1) * 128, :], xpt[b2], txpt[b2], f"dp{b2}", stp[b3], tstp[b3], xbp[b2], txbp[b2],
                      junk3, tjunk3, None, None, bank=0)

            def p3_s0a(s):
                xproc_pe(xbo[s % 2], txbo[s % 2], xTs3[s % 3], txTs3[s % 3], 0)

            def p3_s0b(s):
                xproc_pe(xbp[s % 2], txbp[s % 2], xTps[s % 2], txTps[s % 2], 0)

            def sigm(bank, b3, dst, tdst):
                P.op("scalar", lambda e: e.activation(out=dst[:], in_=PSf[bank], func=AF.Exp, scale=stt[b3][:, 4:5]),
                     r=[PS[bank], tst[b3]], w=[tdst])
                P.op("scalar", lambda e: e.activation(out=dst[:], in_=dst[:], func=AF.Ln, bias=1.0), r=[tdst], w=[tdst])
                P.op("scalar", lambda e: e.activation(out=dst[:], in_=dst[:], func=AF.Exp, scale=-1.0), r=[tdst], w=[tdst])

            def s1_chunks(s):
                b2 = s % 2
                b3 = s % 3
                xT_, txT_, xTp_, txTp_ = xTs3[b3], txTs3[b3], xTps[b2], txTps[b2]

                def B1():
                    proj(xT_, txT_, W3, tW3, O_QA, 512, bank=1)
                    proj(xT_, txT_, W3, tW3, O_KV, 256, bank=2, boff=256, first=True)
                    proj(xTp_, txTp_, W3, tW3, O_KV, 256, bank=2, boff=0, first=False)

                def B2q():
                    qknorm(1, 0, 8, 0, stt[s % 4], tst[s % 4], sq, tsq, tmp, ttmp, f8, tf8, qab, tqab)

                def B3():
                    pv = psb(0)
                    P.grp("tensor", [lambda e, j=j, pv=pv: e.transpose(out=pv[:, j * 128:(j + 1) * 128],
                                                                      in_=qab[:, j * 128:(j + 1) * 128], identity=ident[:])
                                     for j in range(4)], r=[tqab, tC], w=[PS[0]])
                    P.op("vector", lambda e, pv=pv: e.tensor_copy(out=QaT[:].rearrange("p j k -> p (j k)"), in_=pv[:, 0:512]),
                         r=[PS[0]], w=[tQaT])

                def B2k():
                    for kblk in range(2):
                        stx, tstx = (stp[b3], tstp[b3]) if kblk == 0 else (stt[s % 4], tst[s % 4])
                        boff = 0 if kblk == 0 else 256
                        qknorm(2, boff, 2, 1, stx, tstx, sq, tsq, tmp, ttmp, f8, tf8, kab, tkab)
                        for dup in range(2):
                            P.op("gpsimd", lambda e, kblk=kblk, dup=dup: e.tensor_copy(
                                out=kdup[:, kblk, :, dup, :], in_=kab[:, 0:128].rearrange("p (h d) -> p h d", d=64)),
                                r=[tkab], w=[tkdup])
                        P.op("scalar", lambda e, kblk=kblk, boff=boff, stx=stx: e.activation(
                            out=Va[:, kblk, :, 0:64],
                            in_=PSf[2][:, boff + 128:boff + 256].rearrange("p (h d) -> p h d", d=64),
                            func=AF.Copy, scale=stx[:, 3:4]), r=[PS[2], tstx], w=[tVa])

                def B4():
                    pv = psb(0)
                    P.grp("tensor", [lambda e, kblk=kblk, kvh=kvh, pv=pv: e.transpose(
                        out=pv[:, (kblk * 2 + kvh) * 128:(kblk * 2 + kvh + 1) * 128],
                        in_=kdup[:, kblk, kvh, :, :].rearrange("p a d -> p (a d)"), identity=ident[:])
                        for kblk in range(2) for kvh in range(2)], r=[tkdup, tC], w=[PS[0]])
                    P.op("vector", lambda e, pv=pv: e.tensor_copy(out=KaT[:].rearrange("p a b k -> p (a b k)"),
                                                                 in_=pv[:, 0:512]), r=[PS[0]], w=[tKaT])

                def B5():
                    for kblk in range(2):
                        sbk = [4, 5] if kblk == 0 else [1, 2]
                        fns = []
                        for j in range(4):
                            kvh = j // 2
                            for ee in range(2):
                                fns.append(lambda e, kblk=kblk, kvh=kvh, j=j, ee=ee, sbk=sbk: e.matmul(
                                    PSf[sbk[ee]][:, j * 128:(j + 1) * 128],
                                    lhsT=KaT[ee * 64:(ee + 1) * 64, kblk, kvh, :],
                                    rhs=QaT[ee * 64:(ee + 1) * 64, j, :], start=True, stop=True, skip_group_check=True))
                        P.grp("tensor", fns, r=[tKaT, tQaT], w=[PS[sbk[0]], PS[sbk[1]]])
                        for j in range(4):
                            for ee in range(2):
                                hq = 2 * j + ee
                                P.op("scalar", lambda e, kblk=kblk, hq=hq, j=j, ee=ee, sbk=sbk: e.activation(
                                    out=PTa[:, kblk, hq, :], in_=PSf[sbk[ee]][:, j * 128:(j + 1) * 128], func=AF.Exp,
                                    bias=biasA[:, hq, kblk:kblk + 1], scale=0.125),
                                    r=[PS[sbk[ee]], tC], w=[tPTa])
                    P.op("vector", lambda e, s=s: e.scalar_tensor_tensor(
                        out=PTa[:, 0, :, :], in0=PTa[:, 0, :, :], scalar=kvalA[:, s:s + 1],
                        in1=Mprev[:].unsqueeze(1).to_broadcast([128, 8, 128]), op0=ALU.mult, op1=ALU.mult),
                        r=[tC], w=[tPTa])
                    P.op("vector", lambda e: e.tensor_tensor(
                        out=PTa[:, 1, :, :], in0=PTa[:, 1, :, :], in1=Mown[:].unsqueeze(1).to_broadcast([128, 8, 128]),
                        op=ALU.mult), r=[tC], w=[tPTa])

                def B6():
                    fib = {1: True, 2: True}
                    fns = []
                    for hq in range(8):
                        bank = 1 + hq // 4
                        hl = hq % 4
                        kvh = hq // 4
                        for kblk in range(2):
                            sf = fib[bank]
                            fib[bank] = False
                            fns.append(lambda e, hq=hq, hl=hl, kvh=kvh, kblk=kblk, bank=bank, sf=sf: e.matmul(
                                PSf[bank][:, hl * 65:(hl + 1) * 65], lhsT=PTa[:, kblk, hq, :], rhs=Va[:, kblk, kvh, :],
                                start=sf, stop=(kblk == 1), skip_group_check=True))
                    P.grp("tensor", fns, r=[tPTa, tVa], w=[PS[1], PS[2]])
                    for bnk in range(2):
                        P.op("scalar", lambda e, bnk=bnk: e.activation(
                            out=oa[:, bnk * 4:(bnk + 1) * 4, :].rearrange("p a c -> p (a c)"), in_=PSf[1 + bnk][:, 0:260],
                            func=AF.Copy), r=[PS[1 + bnk]], w=[toa])
                    P.op("vector", lambda e: e.tensor_tensor(out=sm[:, 0:8], in0=oa[:, :, 64], in1=sinkexp[:], op=ALU.add),
                         r=[toa, tC], w=[tsm])
                    P.op("vector", lambda e: e.reciprocal(out=sm[:, 8:16], in_=sm[:, 0:8]), r=[tsm], w=[tsm])
                    P.op("vector", lambda e: e.tensor_tensor(
                        out=yas[b2][:].rearrange("p (h d) -> p h d", d=64), in0=oa[:, :, 0:64],
                        in1=sm[:, 8:16].unsqueeze(2).to_broadcast([128, 8, 64]), op=ALU.mult),
                        r=[toa, tsm], w=[tyas[b2]])

                return dict(B1=B1, B2q=B2q, B2k=B2k, B3=B3, B4=B4, B5=B5, B6=B6)

            def s2_chunks(s):
                b2 = s % 2
                b3 = s % 3
                xT_, txT_ = xTs3[b3], txTs3[b3]

                def C1():
                    proj(xT_, txT_, W3, tW3b, O_ZA, 512, bank=6)
                    sigm(6, s % 4, e1, te1)
                    P.op("vector", lambda e: e.scalar_tensor_tensor(
                        out=e2[:], in0=PSf[6], scalar=stt[s % 4][:, 3:4], in1=yas[b2][:], op0=ALU.mult, op1=ALU.mult),
                        r=[PS[6], tst[s % 4], tyas[b2]], w=[te2])
                    P.op("vector", lambda e: e.tensor_tensor(out=yab[:], in0=e2[:], in1=e1[:], op=ALU.mult),
                         r=[te2, te1], w=[tyab])

                def C2():
                    pv = psb(6)
                    fns = []
                    for k in range(8):
                        src = yab[:, k * 128:(k + 1) * 128] if k < 4 else YB[:, s, (k - 4) * 128:(k - 3) * 128]
                        fns.append(lambda e, k=k, src=src, pv=pv: e.transpose(out=pv[:, k * 128:(k + 1) * 128], in_=src,
                                                                             identity=ident[:]))
                    P.grp("tensor", fns, r=[tyab, tYB[s], tC], w=[PS[6]])
                    P.op("vector", lambda e, pv=pv: e.tensor_copy(out=yT[:].rearrange("p k t -> p (k t)"),
                                                                 in_=pv[:, 0:1024]), r=[PS[6]], w=[tyT])

                def C3(g):
                    proj(xT_, txT_, W3, tW3b, O_GL + g * 512, 512, bank=7)
                    proj(xT_, txT_, W3, tW3b, O_GL + 1024 + g * 512, 512, bank=3)
                    sigm(7, s % 4, sgA[g], tsgA[g])
                    sigm(3, s % 4, sgB[g], tsgB[g])

                def C4a(g):
                    P.grp("tensor", [lambda e, k=k, g=g: e.matmul(PSf[6], lhsT=yT[:, k, :],
                                                                 rhs=WUA[:, k, g * 512:(g + 1) * 512],
                                                                 start=(k == 0), stop=(k == 3), skip_group_check=True)
                                     for k in range(4)], r=[tyT, tWU], w=[PS[6]])
                    P.op("vector", lambda e, g=g: e.tensor_tensor(out=sgA[g][:], in0=PSf[6], in1=sgA[g][:], op=ALU.mult),
                         r=[PS[6], tsgA[g]], w=[tsgA[g]])

                def C4b(g):
                    P.grp("tensor", [lambda e, k=k, g=g: e.matmul(PSf[0], lhsT=yT[:, 4 + k, :],
                                                                 rhs=WUB[:, k, g * 512:(g + 1) * 512],
                                                                 start=(k == 0), stop=(k == 3), skip_group_check=True)
                                     for k in range(4)], r=[tyT, tWU], w=[PS[0]])
                    P.op("vector", lambda e, g=g: e.tensor_tensor(out=sgB[g][:], in0=PSf[0], in1=sgB[g][:], op=ALU.mult),
                         r=[PS[0], tsgB[g]], w=[tsgB[g]])
                    P.op("vector", lambda e, g=g: e.tensor_tensor(out=mixb[:, g * 512:(g + 1) * 512], in0=sgA[g][:],
                                                                 in1=sgB[g][:], op=ALU.add),
                         r=[tsgA[g], tsgB[g]], w=[tmixb])

                def C5():
                    pv = psb(6)
                    P.grp("tensor", [lambda e, k=k, pv=pv: e.transpose(out=pv[:, k * 128:(k + 1) * 128],
                                                                      in_=mixb[:, k * 128:(k + 1) * 128], identity=ident[:])
                                     for k in range(8)], r=[tmixb, tC], w=[PS[6]])
                    P.op("vector", lambda e, pv=pv: e.tensor_copy(out=mixT[:].rearrange("p k t -> p (k t)"),
                                                                 in_=pv[:, 0:1024]), r=[PS[6]], w=[tmixT])
                    for g in range(2):
                        bank = 3 if g == 0 else 7
                        P.grp("tensor", [lambda e, k=k, g=g, bank=bank: e.matmul(
                            PSf[bank], lhsT=mixT[:, k, :], rhs=WO[:, k, g * 512:(g + 1) * 512],
                            start=(k == 0), stop=(k == 7), skip_group_check=True) for k in range(8)],
                            r=[tmixT, tWO], w=[PS[bank]])
                        P.op("vector", lambda e, g=g, bank=bank: e.tensor_tensor(
                            out=ot[b2][:, g * 512:(g + 1) * 512], in0=PSf[bank], in1=xt[s % 4][:, g * 512:(g + 1) * 512],
                            op=ALU.add), r=[PS[bank], txt[s % 4]], w=[tot[b2]])
                    P.op("sync", lambda e: e.dma_start(out=out[s * 128:(s + 1) * 128, :], in_=ot[b2][:]),
                         r=[tot[b2]], dsem=f"do{b2}")

                return dict(C1=C1, C2=C2, C3=C3, C4a=C4a, C4b=C4b, C5=C5)

            sgA = [e1, sb(sc, "sgA1", [128, 512], F32)]
            tsgA = [te1, T(None)]
            sgB = [sb(sc, "sgB0", [128, 512], F32), sb(sc, "sgB1", [128, 512], F32)]
            tsgB = [T(None), T(None)]

            N3 = NOWN if nphase >= 3 else 0
            for j in range(N3 + 2):
                A = j if j < N3 else None
                B = s1_chunks(j - 1) if 0 <= j - 1 < N3 else None
                C = s2_chunks(j - 2) if 0 <= j - 2 < N3 else None

                def run(d, name, *args):
                    if d is not None:
                        d[name](*args)

                if j == 0:
                    if N3 > 0:
                        p3_load(0)
                    load_w3a()
                    if N3 > 0:
                        p3_s0a(0)
                        p3_s0b(0)
                run(B, "B1")
                run(B, "B2q")
                run(B, "B2k")
                run(C, "C1")
                if j + 1 < N3:
                    p3_load(j + 1)
                if j == 0:
                    load_w3b()
                if j == 1 or (j == 0 and N3 < 2):
                    load_wuo()
                run(B, "B3")
                run(C, "C3", 0)
                run(C, "C2")
                run(B, "B4")
                run(C, "C4a", 0)
                run(B, "B5")
                run(C, "C4b", 0)
                run(C, "C3", 1)
                run(B, "B6")
                run(C, "C4a", 1)
                run(C, "C4b", 1)
                if j + 1 < N3:
                    p3_s0a(j + 1)
                    p3_s0b(j + 1)
                run(C, "C5")
            P.op("sync", lambda e: e.nop(), w=[tot[0], tot[1]], sig=False)
            P.run_block()
    return nc


def make_in_maps(inputs, NCH=8):
    NB = 1 + 8 * NCH
    x = np.asarray(inputs["x"], dtype=np.float32)
    meta = np.asarray(inputs["meta"], dtype=np.float32)
    B = x.shape[0]
    shared = {}
    for k in ["w_in", "w_up_a", "w_up_b", "w_o"]:
        shared[k] = np.ascontiguousarray(np.asarray(inputs[k], dtype=np.float32)[0])
    for k in ["norm_g", "a_qn", "a_kn", "a_sink", "b_qn", "b_kn", "b_lq1", "b_lk1", "b_lq2", "b_lk2", "b_subln"]:
        shared[k] = np.ascontiguousarray(np.asarray(inputs[k], dtype=np.float32).reshape(1, -1))
    in_maps = []
    blocks = []
    for core in range(8):
        b, c = core // 4, core % 4
        h = np.concatenate([np.zeros((NPAD, D), np.float32), meta, x[b]], axis=0)
        assert h.shape[0] == NB * 128
        own = [1 + 2 * (c + 4 * i) + t for i in range(NCH) for t in range(2)]
        xq = np.concatenate([h[n * 128:(n + 1) * 128] for n in own], axis=0)
        xp = np.concatenate([h[(n - 1) * 128:n * 128] for n in own], axis=0)
        cvec = np.zeros((128, 2), np.float32)
        cvec[:, 0] = 2 * c
        cvec[:, 1] = 128 * 2 * c
        m = dict(shared)
        m.update({"xkv": np.ascontiguousarray(h), "xq": np.ascontiguousarray(xq), "xp": np.ascontiguousarray(xp),
                  "cvec": cvec})
        in_maps.append(m)
        blocks.append((b, own))
    return in_maps, blocks


_CACHE = {}


def kernel(**inputs):
    NCH = 8
    x = np.asarray(inputs["x"])
    B, S, _ = x.shape
    in_maps, blocks = make_in_maps(inputs, NCH)
    if "nc" not in _CACHE:
        _CACHE["nc"] = build_program(NCH)
    nc = _CACHE["nc"]
    res = run_bass_kernel_spmd(nc, in_maps, core_ids=list(range(8)))
    outp = np.zeros((B, S, D), np.float32)
    for core in range(8):
        b, own = blocks[core]
        o = res.results[core]["out"]
        for si, n in enumerate(own):
            outp[b, (n - 1) * 128:n * 128, :] = o[si * 128:(si + 1) * 128, :]
    return outp
```
